# Optimizing a Trainium2 kernel written in Bass

```python
import math
import jax, jax.numpy as jnp
from jax import lax
import numpy as np

D_MODEL = 2048
BATCH = 4
SEQ = 2048
DEPTH = 4
DEC_BATCH = 128
DEC_SEQ = 8
PAST_LEN = 16384
PAGE_SIZE = 128

MIX_WIDTH = D_MODEL
S5_WIDTH = MIX_WIDTH // 2
S5_GROUP = 16
S5_GROUPS = S5_WIDTH // S5_GROUP
S5_STATE = 64
HGRN_WIDTH = MIX_WIDTH - S5_WIDTH
HGRN_DK = 128
HGRN_DV = 128
HGRN_HEADS = HGRN_WIDTH // HGRN_DK
HGRN_CHUNK = 64
D_FF = 128 * ((8 * D_MODEL // 3 + 127) // 128)
IN_WIDTH = S5_WIDTH + 4 * HGRN_WIDTH
EPS = 1e-6
S5_DT_MIN = 1e-3
S5_DT_MAX = 1e-1

kernel_name = 'hymba_s5_hgrn2_macaron_decode_step'


def rms_norm(x, gain):
    x32 = x.astype(jnp.float32)
    y = x32 * lax.rsqrt(jnp.mean(x32 * x32, axis=-1, keepdims=True) + EPS)
    return (y * gain.astype(jnp.float32)).astype(x.dtype)


def swiglu(x, w_gate, w_up, w_down):
    return (jax.nn.silu(x @ w_gate) * (x @ w_up)) @ w_down


def s5_discretise(lam_re, lam_im, log_dt, b_re, b_im):
    lam_re = lam_re.astype(jnp.float32)
    lam_im = lam_im.astype(jnp.float32)
    dt = jnp.exp(log_dt.astype(jnp.float32))[:, None]
    mag = jnp.exp(lam_re * dt)
    ang = lam_im * dt
    abar_re = mag * jnp.cos(ang)
    abar_im = mag * jnp.sin(ang)
    p = abar_re - 1.0
    qv = abar_im
    den = lam_re * lam_re + lam_im * lam_im
    z_re = (p * lam_re + qv * lam_im) / den
    z_im = (qv * lam_re - p * lam_im) / den
    b_re = b_re.astype(jnp.float32)
    b_im = b_im.astype(jnp.float32)
    bb_re = z_re[..., None] * b_re - z_im[..., None] * b_im
    bb_im = z_re[..., None] * b_im + z_im[..., None] * b_re
    return abar_re, abar_im, bb_re, bb_im


def _complex_affine_combine(e1, e2):
    a1r, a1i, b1r, b1i = e1
    a2r, a2i, b2r, b2i = e2
    return (a2r * a1r - a2i * a1i,
            a2r * a1i + a2i * a1r,
            a2r * b1r - a2i * b1i + b2r,
            a2r * b1i + a2i * b1r + b2i)


def s5_scan(u, h_re, h_im, abar_re, abar_im, bb_re, bb_im, c_re, c_im):
    bu_re = jnp.einsum('btgc,gnc->btgn', u, bb_re)
    bu_im = jnp.einsum('btgc,gnc->btgn', u, bb_im)
    init_re = abar_re * h_re - abar_im * h_im
    init_im = abar_re * h_im + abar_im * h_re
    bu_re = bu_re.at[:, 0].add(init_re)
    bu_im = bu_im.at[:, 0].add(init_im)
    a_re = jnp.broadcast_to(abar_re, bu_re.shape)
    a_im = jnp.broadcast_to(abar_im, bu_im.shape)
    _, _, x_re, x_im = lax.associative_scan(_complex_affine_combine,
                                            (a_re, a_im, bu_re, bu_im), axis=1)
    y = (jnp.einsum('btgn,gcn->btgc', x_re, c_re.astype(jnp.float32))
         - jnp.einsum('btgn,gcn->btgc', x_im, c_im.astype(jnp.float32)))
    return y, x_re[:, -1], x_im[:, -1]


def hgrn_lower_bounds(lb_param):
    p = jax.nn.softmax(lb_param.astype(jnp.float32), axis=0)
    return jnp.cumsum(p, axis=0) - p[0]


def chunk_gla(q, k, v, log_f, s0):
    B, T, H, dk = q.shape
    dv = v.shape[-1]
    C = math.gcd(T, HGRN_CHUNK)
    n = T // C

    def to_chunks(t):
        return t.reshape(B, n, C, H, t.shape[-1]).transpose(1, 0, 3, 2, 4)

    causal = jnp.tril(jnp.ones((C, C), dtype=bool))[:, :, None]

    def step(S, inp):
        qi, ki, vi, fi = inp
        b = jnp.cumsum(fi, axis=2)
        o_inter = jnp.einsum('bhtk,bhkv->bhtv', qi * jnp.exp(b), S)
        diff = b[:, :, :, None, :] - b[:, :, None, :, :]
        decay = jnp.where(causal, jnp.exp(jnp.where(causal, diff, 0.0)), 0.0)
        att = jnp.einsum('bhtsk,bhsk->bhts', qi[:, :, :, None, :] * decay, ki)
        o = o_inter + jnp.einsum('bhts,bhsv->bhtv', att, vi)
        b_last = b[:, :, -1:, :]
        S = (jnp.exp(b_last[:, :, 0, :])[..., None] * S
             + jnp.einsum('bhsk,bhsv->bhkv', ki * jnp.exp(b_last - b), vi))
        return S, o

    S, o = lax.scan(step, s0, (to_chunks(q), to_chunks(k), to_chunks(v), to_chunks(log_f)))
    o = o.transpose(1, 0, 3, 2, 4).reshape(B, T, H, dv)
    return o, S


def hgrn2(q, f, v, g, lb, s0, gain):
    B, T, _ = q.shape
    shp = (B, T, HGRN_HEADS, HGRN_DK)
    qh = jax.nn.silu(q.astype(jnp.float32)).reshape(shp)
    f_gate = lb + (1.0 - lb) * jax.nn.sigmoid(f.astype(jnp.float32))
    log_f = jnp.log(f_gate).reshape(shp)
    k = (1.0 - f_gate).reshape(shp)
    vh = v.astype(jnp.float32).reshape(B, T, HGRN_HEADS, HGRN_DV)
    o, s_new = chunk_gla(qh, k, vh, log_f, s0)
    o = o * lax.rsqrt(jnp.mean(o * o, axis=-1, keepdims=True) + EPS)
    o = o * gain.astype(jnp.float32).reshape(HGRN_HEADS, HGRN_DV)
    o = o.reshape(B, T, HGRN_WIDTH) * jax.nn.silu(g.astype(jnp.float32))
    return o, s_new


def mixer(a, s_re, s_im, s_h, lb, p, l):
    B, T, _ = a.shape
    z = a @ p['w_in'][l]
    u, q, f, v, g = jnp.split(z, [S5_WIDTH, S5_WIDTH + HGRN_WIDTH,
                                  S5_WIDTH + 2 * HGRN_WIDTH,
                                  S5_WIDTH + 3 * HGRN_WIDTH], axis=-1)
    abar_re, abar_im, bb_re, bb_im = s5_discretise(p['s5_lam_re'][l], p['s5_lam_im'][l],
                                                   p['s5_log_dt'][l], p['s5_b_re'][l],
                                                   p['s5_b_im'][l])
    u32 = u.astype(jnp.float32)
    ys, s_re_new, s_im_new = s5_scan(u32.reshape(B, T, S5_GROUPS, S5_GROUP),
                                     s_re.astype(jnp.float32), s_im.astype(jnp.float32),
                                     abar_re, abar_im, bb_re, bb_im,
                                     p['s5_c_re'][l], p['s5_c_im'][l])
    ys = ys.reshape(B, T, S5_WIDTH) + p['s5_d'][l].astype(jnp.float32) * u32
    ys = jax.nn.gelu(ys).astype(a.dtype)
    ys = ys * jax.nn.sigmoid(ys @ p['s5_w_glu'][l] + p['s5_b_glu'][l])
    yh, s_h_new = hgrn2(q, f, v, g, lb, s_h.astype(jnp.float32), p['hgrn_norm'][l])
    out = jnp.concatenate([ys, yh.astype(a.dtype)], axis=-1) @ p['w_out'][l]
    return out, s_re_new, s_im_new, s_h_new


def trunk(x, s5_re0, s5_im0, hgrn0, p):
    lb_all = hgrn_lower_bounds(p['hgrn_lb'])
    h = x
    new_re, new_im, new_h = [], [], []
    for l in range(DEPTH):
        a = swiglu(rms_norm(h, p['norm_pre'][l, 0]), p['ffn1_w_gate'][l],
                   p['ffn1_w_up'][l], p['ffn1_w_down'][l])
        h = h + 0.5 * rms_norm(a, p['norm_post'][l, 0])
        a, sre, sim, sh = mixer(rms_norm(h, p['norm_pre'][l, 1]), s5_re0[l], s5_im0[l],
                                hgrn0[l], lb_all[l], p, l)
        h = h + rms_norm(a, p['norm_post'][l, 1])
        a = swiglu(rms_norm(h, p['norm_pre'][l, 2]), p['ffn2_w_gate'][l],
                   p['ffn2_w_up'][l], p['ffn2_w_down'][l])
        h = h + 0.5 * rms_norm(a, p['norm_post'][l, 2])
        new_re.append(sre)
        new_im.append(sim)
        new_h.append(sh)
    return h, jnp.stack(new_re), jnp.stack(new_im), jnp.stack(new_h)


def setup_inputs(seed: int = 0) -> dict:
    key = jax.random.key(seed)
    ks = jax.random.split(key, 32)
    f32 = jnp.float32
    nrm = lambda k, shape, s: jax.random.normal(k, shape, f32) * s
    G, N, Cg = S5_GROUPS, S5_STATE, S5_GROUP
    lam_im = (jnp.pi * jnp.arange(N, dtype=f32))[None, None, :] + nrm(ks[20], (DEPTH, G, N), 0.01)
    return {
        'x_prompt': nrm(ks[0], (BATCH, SEQ, D_MODEL), 1.0),
        'x_sample': nrm(ks[1], (DEC_BATCH, DEC_SEQ, D_MODEL), 1.0),
        'state_s5_re': nrm(ks[2], (DEPTH, DEC_BATCH, G, N), 0.3),
        'state_s5_im': nrm(ks[3], (DEPTH, DEC_BATCH, G, N), 0.3),
        'state_hgrn': nrm(ks[4], (DEPTH, DEC_BATCH, HGRN_HEADS, HGRN_DK, HGRN_DV), 0.5),
        'norm_pre': 1.0 + nrm(ks[5], (DEPTH, 3, D_MODEL), 0.02),
        'norm_post': 1.0 + nrm(ks[6], (DEPTH, 3, D_MODEL), 0.02),
        'ffn1_w_gate': nrm(ks[7], (DEPTH, D_MODEL, D_FF), D_MODEL ** -0.5),
        'ffn1_w_up': nrm(ks[8], (DEPTH, D_MODEL, D_FF), D_MODEL ** -0.5),
        'ffn1_w_down': nrm(ks[9], (DEPTH, D_FF, D_MODEL), D_FF ** -0.5),
        'ffn2_w_gate': nrm(ks[10], (DEPTH, D_MODEL, D_FF), D_MODEL ** -0.5),
        'ffn2_w_up': nrm(ks[11], (DEPTH, D_MODEL, D_FF), D_MODEL ** -0.5),
        'ffn2_w_down': nrm(ks[12], (DEPTH, D_FF, D_MODEL), D_FF ** -0.5),
        'w_in': nrm(ks[13], (DEPTH, D_MODEL, IN_WIDTH), D_MODEL ** -0.5),
        'w_out': nrm(ks[14], (DEPTH, MIX_WIDTH, D_MODEL), MIX_WIDTH ** -0.5),
        's5_lam_re': -0.5 + nrm(ks[15], (DEPTH, G, N), 0.01),
        's5_lam_im': lam_im,
        's5_log_dt': jax.random.uniform(ks[16], (DEPTH, G), f32,
                                        math.log(S5_DT_MIN), math.log(S5_DT_MAX)),
        's5_b_re': nrm(ks[17], (DEPTH, G, N, Cg), (2 * Cg) ** -0.5),
        's5_b_im': nrm(ks[18], (DEPTH, G, N, Cg), (2 * Cg) ** -0.5),
        's5_c_re': nrm(ks[19], (DEPTH, G, Cg, N), N ** -0.5),
        's5_c_im': nrm(ks[21], (DEPTH, G, Cg, N), N ** -0.5),
        's5_d': nrm(ks[22], (DEPTH, S5_WIDTH), 1.0),
        's5_w_glu': nrm(ks[23], (DEPTH, S5_WIDTH, S5_WIDTH), S5_WIDTH ** -0.5),
        's5_b_glu': nrm(ks[24], (DEPTH, S5_WIDTH), 0.01),
        'hgrn_lb': nrm(ks[25], (DEPTH, HGRN_WIDTH), 0.1),
        'hgrn_norm': 1.0 + nrm(ks[26], (DEPTH, HGRN_WIDTH), 0.02),
    }


def reference(x_prompt, x_sample, state_s5_re, state_s5_im, state_hgrn,
              norm_pre, norm_post,
              ffn1_w_gate, ffn1_w_up, ffn1_w_down,
              ffn2_w_gate, ffn2_w_up, ffn2_w_down,
              w_in, w_out,
              s5_lam_re, s5_lam_im, s5_log_dt, s5_b_re, s5_b_im, s5_c_re, s5_c_im,
              s5_d, s5_w_glu, s5_b_glu,
              hgrn_lb, hgrn_norm):
    p = dict(norm_pre=norm_pre, norm_post=norm_post,
             ffn1_w_gate=ffn1_w_gate, ffn1_w_up=ffn1_w_up, ffn1_w_down=ffn1_w_down,
             ffn2_w_gate=ffn2_w_gate, ffn2_w_up=ffn2_w_up, ffn2_w_down=ffn2_w_down,
             w_in=w_in, w_out=w_out,
             s5_lam_re=s5_lam_re, s5_lam_im=s5_lam_im, s5_log_dt=s5_log_dt,
             s5_b_re=s5_b_re, s5_b_im=s5_b_im, s5_c_re=s5_c_re, s5_c_im=s5_c_im,
             s5_d=s5_d, s5_w_glu=s5_w_glu, s5_b_glu=s5_b_glu,
             hgrn_lb=hgrn_lb, hgrn_norm=hgrn_norm)
    sdt = state_hgrn.dtype
    z_s5 = jnp.zeros((DEPTH, BATCH, S5_GROUPS, S5_STATE), jnp.float32)
    z_h = jnp.zeros((DEPTH, BATCH, HGRN_HEADS, HGRN_DK, HGRN_DV), jnp.float32)
    y_prompt, re_p, im_p, h_p = trunk(x_prompt, z_s5, z_s5, z_h, p)
    y_sample, re_s, im_s, h_s = trunk(x_sample, state_s5_re, state_s5_im, state_hgrn, p)
    return (y_prompt, y_sample,
            re_p.astype(sdt), im_p.astype(sdt), h_p.astype(sdt),
            re_s.astype(sdt), im_s.astype(sdt), h_s.astype(sdt))
```

```python
import contextlib
import numpy as np
import concourse.bass as bass
import concourse.mybir as mybir
from concourse.bass_utils import run_bass_kernel_spmd

F32 = mybir.dt.float32
BF16 = mybir.dt.bfloat16
AF = mybir.ActivationFunctionType
ALU = mybir.AluOpType

D = 2048
KC = 16
DFF = 5504
MFF = 43
DEPTH = 4
NCORES = 8
SEQ = 2048
NSAMP = 16
DSEQ = 8
TP = 512
TSQ = 4
TS = TSQ * DSEQ
TT = TP + TS
HALF = TT // 2
NTILES = SEQ // TP
EPS = 1e-6
NSLOT = 8


class Sched:
    EPOCH = 20000

    def __init__(self, nc, stack):
        self.nc = nc
        self.stack = stack
        self.engs = {'pe': nc.tensor, 'act': nc.scalar, 'dve': nc.vector,
                     'pool': nc.gpsimd, 'sp': nc.sync}
        self.cur = {}
        self.seen = {e: {} for e in self.engs}
        self.res = {}
        self.nsem = 0
        self.dma_pool = {}
        self.ninst = 0

    def _new_sem(self, name):
        s = self.stack.enter_context(self.nc.semaphore(name))
        self.nsem += 1
        return s

    def _eng_sem(self, e):
        c = self.cur.get(e)
        if c is None or c[2] >= self.EPOCH:
            ep = 0 if c is None else c[1][1] + 1
            c = [self._new_sem(f"s_{e}_{ep}"), (e, ep), 0]
            self.cur[e] = c
        return c

    def _wait(self, e, tickets):
        need = {}
        for t, raw in tickets:
            if t is None:
                continue
            key, sem, val = t
            if key[0] == e and (e == 'pe' or not raw):
                continue
            if self.seen[e].get(key, 0) >= val:
                continue
            if key not in need or need[key][1] < val:
                need[key] = (sem, val)
        for key, (sem, val) in need.items():
            self.engs[e].wait_ge(sem, val)
            self.seen[e][key] = val
            self.ninst += 1

    def _deps(self, reads, writes):
        deps = []
        for r in reads:
            st = self.res.get(r)
            if st:
                deps.append((st[0], True))
        for w in writes:
            st = self.res.get(w)
            if st:
                deps.append((st[0], False))
                deps.extend((x, False) for x in st[1])
        return deps

    def _commit(self, ticket, reads, writes):
        for r in reads:
            st = self.res.setdefault(r, [None, []])
            st[1].append(ticket)
        for w in writes:
            self.res[w] = [ticket, []]

    def op(self, e, fn, reads=(), writes=()):
        return self.group(e, [fn], reads, writes)

    def group(self, e, fns, reads=(), writes=()):
        self._wait(e, self._deps(reads, writes))
        c = self._eng_sem(e)
        for fn in fns[:-1]:
            fn()
        inst = fns[-1]()
        inst.then_inc(c[0], 1)
        self.ninst += len(fns)
        c[2] += 1
        t = (c[1], c[0], c[2])
        self._commit(t, reads, writes)
        return t

    def dma(self, e, out, in_, reads=(), writes=(), nsem=8, **kw):
        pool = self.dma_pool.get(e)
        if pool is None:
            pool = {'slots': [[self._new_sem(f"d_{e}_{i}"), ('dma', e, i), 0, None]
                              for i in range(nsem)], 'rr': 0}
            self.dma_pool[e] = pool
        slot = pool['slots'][pool['rr'] % len(pool['slots'])]
        pool['rr'] += 1
        deps = self._deps(reads, writes)
        if slot[3] is not None:
            deps.append((slot[3], True))
        self._wait(e, deps)
        inst = self.engs[e].dma_start(out=out, in_=in_, **kw)
        inst.then_inc(slot[0], 16)
        self.ninst += 1
        slot[2] += 16
        t = (slot[1], slot[0], slot[2])
        slot[3] = t
        self._commit(t, reads, writes)
        return t

    def barrier(self, engs=('pe', 'act', 'dve', 'sp')):
        ts = []
        for e2, c in self.cur.items():
            if c[2] > 0:
                ts.append(((c[1], c[0], c[2]), True))
        p = self.dma_pool.get('sp')
        if p:
            for sl in p['slots']:
                if sl[3] is not None:
                    ts.append((sl[3], True))
        for e in engs:
            self._wait(e, ts)

    def wait_all(self, e, keys):
        deps = []
        for k in keys:
            st = self.res.get(k)
            if st:
                deps.append((st[0], True))
                deps.extend((x, True) for x in st[1])
        self._wait(e, deps)


import math

NT = 68
HG_SPLIT = 288
TWO_PI = 2.0 * math.pi


def _sz(dt):
    return 4 if dt == F32 or dt == mybir.dt.int32 else 2


class Region:
    def __init__(self, t, nbytes):
        self.t, self.n, self.off = t, nbytes, 0

    def reset(self, off=0):
        self.off = off

    def alloc(self, free, dt, parts=128):
        n = 1
        for f in free:
            n *= f
        nb = n * _sz(dt)
        nb = (nb + 3) // 4 * 4
        assert self.off + nb <= self.n, (self.off, nb, self.n)
        ap = self.t[0:parts, self.off // 2:(self.off + nb) // 2]
        self.off += nb
        if dt != BF16:
            ap = ap.bitcast(dt)
        if len(free) == 2:
            ap = ap.rearrange("p (a b) -> p a b", b=free[1])
        elif len(free) == 3:
            ap = ap.rearrange("p (a b c) -> p a b c", b=free[1], c=free[2])
        elif len(free) == 4:
            ap = ap.rearrange("p (a b c d) -> p a b c d", b=free[1], c=free[2], d=free[3])
        return ap


class Prog:
    def __init__(self, cfg):
        self.cfg = cfg
        self.nc = bass.Bass("TRN2", target_bir_lowering=False)
        self.stack = contextlib.ExitStack()
        self.S = Sched(self.nc, self.stack)
        self.bank_rr = 0
        self.slot_rr = 0
        self.stage_rr = 0
        self.tmp_rr = 0
        self.uid = 0
        self.out_keys = []

    def din(self, name, shape, dt=F32):
        return self.nc.dram_tensor(name, list(shape), dt, kind="ExternalInput").ap()

    def dout(self, name, shape, dt=F32):
        return self.nc.dram_tensor(name, list(shape), dt, kind="ExternalOutput").ap()

    def dscr(self, name, shape, dt):
        return self.nc.dram_tensor(name, list(shape), dt, kind="Internal").ap()

    def sb(self, name, shape, dt):
        return self.stack.enter_context(self.nc.sbuf_tensor(name, list(shape), dt))

    def ps(self, name, shape, dt=F32):
        return self.stack.enter_context(self.nc.psum_tensor(name, list(shape), dt))

    nbanks_rot = 8

    def bank(self):
        i = self.bank_rr % self.nbanks_rot
        self.bank_rr += 1
        return i

    def tmp(self):
        i = self.tmp_rr % len(self.tmpa)
        self.tmp_rr += 1
        return i, self.tmpa[i]

    def key(self, s):
        self.uid += 1
        return (s, self.uid)

    def build(self):
        nc, S, cfg = self.nc, self.S, self.cfg
        L = cfg.get('layers', DEPTH)
        ntiles = cfg.get('ntiles', NTILES)
        self.L, self.ntiles = L, ntiles
        self.do_mixer = cfg.get('mixer', True)
        self.xp = self.din("xp", [SEQ, D])
        self.xs = self.din("xs", [NSAMP * DSEQ, D])
        self.norm_pre = self.din("norm_pre", [DEPTH, 3, D])
        self.norm_post = self.din("norm_post", [DEPTH, 3, D])
        self.w = {}
        for nm, shp in [("ffn1_w_gate", [DEPTH, D, DFF]), ("ffn1_w_up", [DEPTH, D, DFF]),
                        ("ffn1_w_down", [DEPTH, DFF, D]), ("ffn2_w_gate", [DEPTH, D, DFF]),
                        ("ffn2_w_up", [DEPTH, D, DFF]), ("ffn2_w_down", [DEPTH, DFF, D]),
                        ("w_in", [DEPTH, D, 5120]), ("w_out", [DEPTH, D, D]),
                        ("s5_w_glu", [DEPTH, 1024, 1024]),
                        ("s5_lam_re", [DEPTH, 64, 64]), ("s5_lam_im", [DEPTH, 64, 64]),
                        ("s5_log_dt", [DEPTH, 64]),
                        ("s5_b_re", [DEPTH, 64, 64, 16]), ("s5_b_im", [DEPTH, 64, 64, 16]),
                        ("s5_c_re", [DEPTH, 64, 16, 64]), ("s5_c_im", [DEPTH, 64, 16, 64]),
                        ("s5_d", [DEPTH, 1024]), ("s5_b_glu", [DEPTH, 1024]),
                        ("hgrn_lb", [DEPTH, 1024]), ("hgrn_norm", [DEPTH, 1024]),
                        ("s5re_in", [DEPTH, NSAMP, 64, 64]), ("s5im_in", [DEPTH, NSAMP, 64, 64]),
                        ("hg_in", [DEPTH, NSAMP, 8, 128, 128])]:
            self.w[nm] = self.din(nm, shp)
        self.yp = self.dout("yp", [SEQ, D])
        self.ys = self.dout("ys", [NSAMP * DSEQ, D])
        self.o_s5p = [self.dout("s5re_p", [DEPTH, 64, 64]), self.dout("s5im_p", [DEPTH, 64, 64])]
        self.o_hgp = self.dout("hg_p", [DEPTH, 8, 128, 128])
        self.o_s5s = [self.dout("s5re_s", [DEPTH, NSAMP, 64, 64]), self.dout("s5im_s", [DEPTH, NSAMP, 64, 64])]
        self.o_hgs = self.dout("hg_s", [DEPTH, NSAMP, 8, 128, 128])
        self.s5bt = self.dscr("s5bt", [DEPTH, 8, 2, 2, 128, 2048], BF16)
        self.s5ct = self.dscr("s5ct", [DEPTH, 8, 2, 2, 128, 2048], BF16)
        self.hgscr = self.dscr("hgscr", [DEPTH, 128, 1024], F32)

        HB, XB, AB, MB = KC * TT * 4, KC * TT * 2, KC * TT * 4, MFF * TT * 2
        self.Hb = self.sb("Hb", [128, HB // 2], BF16)
        self.Xb = self.sb("Xb", [128, XB // 2], BF16)
        self.Ab = self.sb("Ab", [128, AB // 2], BF16)
        self.Mb = self.sb("Mb", [128, MB // 2], BF16)
        self.RH, self.RX = Region(self.Hb, HB), Region(self.Xb, XB)
        self.RA, self.RM = Region(self.Ab, AB), Region(self.Mb, MB)
        self.h = self.Hb[:, :].bitcast(F32).rearrange("p (c t) -> p c t", t=TT)
        self.xn = self.Xb[:, :].rearrange("p (c t) -> p c t", t=TT)
        self.abuf = self.Ab[:, :].bitcast(F32).rearrange("p (c t) -> p c t", t=TT)
        self.mid = self.Mb[:, :].rearrange("p (c t) -> p c t", t=TT)
        self.stage = [self.Mb[:, i * 4096:(i + 1) * 4096].bitcast(F32) for i in range(2)]
        self.slots = [self.sb(f"slot{i}", [128, KC, 128], BF16) for i in range(NSLOT)]
        self.ident = self.sb("ident", [128, 128], F32)
        self.ones_f = self.sb("ones_f", [128, 128], F32)
        self.ones_b = self.sb("ones_b", [128, 128], BF16)
        self.epsb = self.sb("epsb", [128, 1], F32)
        self.gpre = self.sb("gpre", [128, DEPTH * 3 * KC], F32)
        self.gpost = self.sb("gpost", [128, DEPTH * 3 * KC], F32)
        self.tmpa = [self.sb(f"tmpa{i}", [128, HALF], F32) for i in range(4)]
        self.sq4 = [self.sb(f"sq4_{i}", [128, 4, HALF], BF16) for i in range(2)]
        self.sq_rr = 0
        self.rstd = self.sb("rstd", [128, TT], F32)
        self.banks = [self.ps(f"bank{i}", [128, 512], F32) for i in range(8)]
        self.tabA = self.sb("tabA", [128, DEPTH, 4, 32], F32)
        self.fvec = self.sb("fvec", [128, 5, 32], F32)
        self.s5carry = self.sb("s5carry", [128, DEPTH, 2, 32], F32)
        self.mask8 = self.sb("mask8", [128, TT], F32)
        self.mask32 = self.sb("mask32", [128, TT], F32)
        self.maskc = self.sb("maskc", [32, 32], F32)
        self.sst = [self.sb(f"sst{i}", [128, 128], F32) for i in range(2)]

        self.init_consts()
        if self.do_mixer:
            self.prologue()
        S.barrier(('pe', 'act', 'dve', 'sp'))
        for t in range(ntiles):
            self.load_tile(t)
            for l in range(L):
                self.ffn(l, 0)
                if self.do_mixer:
                    S.barrier()
                    self.mixer(t, l)
                    S.barrier()
                if cfg.get('ffn2', True):
                    self.ffn(l, 2)
            S.barrier()
            self.store_tile(t)
            S.barrier()
        self.finish()
        return nc

    def init_consts(self):
        nc, S = self.nc, self.S
        S.op('pool', lambda: nc.gpsimd.memset(self.ones_f[:], 1.0), writes=['ones_f'])
        S.op('pool', lambda: nc.gpsimd.memset(self.ones_b[:], 1.0), writes=['ones_b'])
        S.op('pool', lambda: nc.gpsimd.memset(self.epsb[:], EPS), writes=['epsb'])
        S.op('pool', lambda: nc.gpsimd.affine_select(
            out=self.ident[:], in_=self.ones_f[:], pattern=[[-1, 128]],
            compare_op=ALU.is_equal, fill=0.0, base=0, channel_multiplier=1),
            reads=['ones_f'], writes=['ident'])
        for (src, dst, key) in ((self.norm_pre, self.gpre, 'gpre'), (self.norm_post, self.gpost, 'gpost')):
            rows = src.rearrange("l j (c p) -> (l j c) p", p=128)
            for hf in range(2):
                self.rows_to_cols(rows[hf * 96:(hf + 1) * 96, :], 96, dst[:, hf * 96:(hf + 1) * 96], (key, hf))

    def rows_to_cols(self, rows, n, dst, wkey, eng='act'):
        nc, S = self.nc, self.S
        si = self.stage_rr % 2
        self.stage_rr += 1
        st = self.stage[si]
        S.dma('sp', st[0:n, 0:128], rows, writes=[('stage', si)])
        bi = self.bank()
        pb = self.banks[bi]
        S.group('pe', [lambda: nc.tensor.transpose(pb[:, 0:n], st[0:n, 0:128], self.ident[0:n, 0:n])],
                reads=[('stage', si), 'ident'], writes=[('bank', bi)])
        S.op('act', lambda: nc.scalar.copy(out=dst, in_=pb[:, 0:n]), writes=[('bank', bi), wkey])

    def hkeys(self, cs_list):
        return [('h', c, hf) for c in cs_list for hf in range(2)]

    def tile_blocks(self, t, pa, sa):
        blocks = [(sa[t * TS:(t + 1) * TS, :], TS, 0)]
        blocks += [(pa[t * TP + b * 128: t * TP + (b + 1) * 128, :], 128, TS + b * 128) for b in range(4)]
        return blocks

    def load_tile(self, t):
        nc, S = self.nc, self.S
        for src, n, col0 in self.tile_blocks(t, self.xp, self.xs):
            si = self.stage_rr % 2
            self.stage_rr += 1
            st = self.stage[si]
            S.dma('sp', st[0:n, :], src, writes=[('stage', si)])
            for c4 in range(KC // 4):
                bi = self.bank()
                pb = self.banks[bi]
                S.group('pe', [
                    (lambda cc=c4 * 4 + j, j=j, pb=pb, st=st, n=n: nc.tensor.transpose(
                        pb[:, j * 128: j * 128 + n], st[0:n, cc * 128:(cc + 1) * 128],
                        self.ident[0:n, 0:n]))
                    for j in range(4)], reads=[('stage', si), 'ident'], writes=[('bank', bi)])
                S.op('act', lambda pb=pb, c4=c4, col0=col0, n=n: nc.scalar.copy(
                    out=self.h[:, c4 * 4:(c4 + 1) * 4, col0:col0 + n],
                    in_=pb[:, 0:512].rearrange("p (j n) -> p j n", n=128)[:, :, 0:n]),
                    writes=[('bank', bi)] + self.hkeys(range(c4 * 4, c4 * 4 + 4)))

    def store_tile(self, t):
        nc, S = self.nc, self.S
        for bidx, (dst, n, col0) in enumerate(self.tile_blocks(t, self.yp, self.ys)):
            si = self.stage_rr % 2
            self.stage_rr += 1
            st = self.stage[si]
            for c4 in range(KC // 4):
                bi = self.bank()
                pb = self.banks[bi]
                S.group('pe', [
                    (lambda cc=c4 * 4 + j, j=j, pb=pb, n=n, col0=col0: nc.tensor.transpose(
                        pb[0:n, j * 128:(j + 1) * 128], self.h[:, cc, col0:col0 + n],
                        self.ident[:, :]))
                    for j in range(4)],
                    reads=self.hkeys(range(c4 * 4, c4 * 4 + 4)) + ['ident'], writes=[('bank', bi)])
                S.op('act', lambda pb=pb, st=st, c4=c4, n=n: nc.scalar.copy(
                    out=st[0:n, c4 * 512:(c4 + 1) * 512], in_=pb[0:n, 0:512]),
                    writes=[('bank', bi), ('stage', si)])
            ok = ('out', t, bidx)
            S.dma('sp', dst, st[0:n, :], reads=[('stage', si)], writes=[ok])
            self.out_keys.append(ok)

    def finish(self):
        self.S.wait_all('sp', self.out_keys)

    def load_unit(self, src_rows_cols, nk):
        nc, S = self.nc, self.S
        si = self.slot_rr % NSLOT
        self.slot_rr += 1
        sl = self.slots[si]
        S.dma('pool', sl[:, 0:nk, :], src_rows_cols.rearrange("(kc p) m -> p kc m", p=128),
              writes=[('slot', si)])
        return si

    def load_s5unit(self, src):
        S = self.S
        si = self.slot_rr % NSLOT
        self.slot_rr += 1
        sl = self.slots[si]
        S.dma('sp', sl[:, :, :], src.rearrange("p (a b) -> p a b", b=128),
              reads=['s5scr'], writes=[('slot', si)])
        return si

    def rms_stats(self, src, keyfn):
        nc, S = self.nc, self.S
        for half in range(2):
            cs = slice(half * HALF, (half + 1) * HALF)
            bi = self.bank()
            pb = self.banks[bi]
            for c4 in range(KC // 4):
                qi = self.sq_rr % 2
                self.sq_rr += 1
                sq = self.sq4[qi]
                S.op('act', lambda: nc.scalar.activation(
                    out=sq[:, :, :], in_=src[:, c4 * 4:(c4 + 1) * 4, cs], func=AF.Square),
                    reads=[keyfn(c, half) for c in range(c4 * 4, c4 * 4 + 4)], writes=[('sq4', qi)])
                S.group('pe', [
                    (lambda j=j: nc.tensor.matmul(
                        pb[:, 0:HALF], self.ones_b[:], sq[:, j, :],
                        start=(c4 == 0 and j == 0), stop=(c4 == KC // 4 - 1 and j == 3)))
                    for j in range(4)], reads=[('sq4', qi), 'ones_b'], writes=[('bank', bi)])
            ti, tm = self.tmp()
            S.op('act', lambda tm=tm, pb=pb: nc.scalar.activation(
                out=tm[:], in_=pb[:, 0:HALF], func=AF.Sqrt, bias=self.epsb[:], scale=1.0 / D),
                reads=['epsb'], writes=[('bank', bi), ('tmpa', ti)])
            S.op('dve', lambda tm=tm, cs=cs: nc.vector.reciprocal(self.rstd[:, cs], tm[:]),
                 reads=[('tmpa', ti)], writes=[('rstd', half)])

    def pre_norm(self, l, j):
        nc, S = self.nc, self.S
        self.rms_stats(self.h, lambda c, hf: ('h', c, hf))
        g = self.gpre
        gi = (l * 3 + j) * KC
        for half in range(2):
            cs = slice(half * HALF, (half + 1) * HALF)
            for c in range(KC):
                S.op('dve', lambda c=c, cs=cs: nc.vector.scalar_tensor_tensor(
                    out=self.xn[:, c, cs], in0=self.h[:, c, cs], scalar=g[:, gi + c:gi + c + 1],
                    in1=self.rstd[:, cs], op0=ALU.mult, op1=ALU.mult),
                    reads=[('h', c, half), ('gpre', 0), ('gpre', 1), ('rstd', half)],
                    writes=[('xn', c, half)])

    def post_norm(self, l, j, coef):
        nc, S = self.nc, self.S
        self.rms_stats(self.abuf, lambda c, hf: ('abuf', c, hf))
        g = self.gpost
        gi = (l * 3 + j) * KC
        for half in range(2):
            cs = slice(half * HALF, (half + 1) * HALF)
            for c in range(KC):
                ti, tm = self.tmp()
                S.op('dve', lambda c=c, tm=tm, cs=cs: nc.vector.scalar_tensor_tensor(
                    out=tm[:], in0=self.abuf[:, c, cs], scalar=g[:, gi + c:gi + c + 1],
                    in1=self.rstd[:, cs], op0=ALU.mult, op1=ALU.mult),
                    reads=[('abuf', c, half), ('gpost', 0), ('gpost', 1), ('rstd', half)],
                    writes=[('tmpa', ti)])
                S.op('dve', lambda c=c, tm=tm, cs=cs: nc.vector.scalar_tensor_tensor(
                    out=self.h[:, c, cs], in0=tm[:], scalar=float(coef),
                    in1=self.h[:, c, cs], op0=ALU.mult, op1=ALU.add),
                    reads=[('tmpa', ti)], writes=[('h', c, half)])

    def proj(self, wsrc, nk, rhs_fn, rkeys_fn, evac):
        nc, S = self.nc, self.S
        si = self.load_unit(wsrc, nk)
        sl = self.slots[si]
        for half in range(2):
            cs = slice(half * HALF, (half + 1) * HALF)
            bi = self.bank()
            pb = self.banks[bi]
            S.group('pe', [
                (lambda kc=kc, pb=pb, cs=cs: nc.tensor.matmul(
                    pb[:, 0:HALF], sl[:, kc, :], rhs_fn(kc, cs), start=(kc == 0), stop=(kc == nk - 1)))
                for kc in range(nk)],
                reads=[('slot', si)] + rkeys_fn(half), writes=[('bank', bi)])
            evac(half, cs, pb, bi)

    def ffn(self, l, j):
        nc, S = self.nc, self.S
        pfx = "ffn1" if j == 0 else "ffn2"
        wg, wu, wd = self.w[pfx + "_w_gate"], self.w[pfx + "_w_up"], self.w[pfx + "_w_down"]
        self.pre_norm(l, j)
        for m in range(MFF):
            sg = self.load_unit(wg[l, :, m * 128:(m + 1) * 128], KC)
            su = self.load_unit(wu[l, :, m * 128:(m + 1) * 128], KC)
            for half in range(2):
                cs = slice(half * HALF, (half + 1) * HALF)
                bg, bu = self.bank(), self.bank()
                pg, pu = self.banks[bg], self.banks[bu]
                for (si, pb, bi) in ((sg, pg, bg), (su, pu, bu)):
                    sl = self.slots[si]
                    S.group('pe', [
                        (lambda kc=kc, sl=sl, pb=pb, cs=cs: nc.tensor.matmul(
                            pb[:, 0:HALF], sl[:, kc, :], self.xn[:, kc, cs],
                            start=(kc == 0), stop=(kc == KC - 1)))
                        for kc in range(KC)],
                        reads=[('slot', si)] + [('xn', c, half) for c in range(KC)],
                        writes=[('bank', bi)])
                ti, tm = self.tmp()
                S.op('act', lambda tm=tm, pg=pg: nc.scalar.activation(
                    out=tm[:], in_=pg[:, 0:HALF], func=AF.Silu),
                    writes=[('bank', bg), ('tmpa', ti)])
                S.op('dve', lambda tm=tm, pu=pu, m=m, cs=cs: nc.vector.tensor_tensor(
                    out=self.mid[:, m, cs], in0=tm[:], in1=pu[:, 0:HALF], op=ALU.mult),
                    reads=[('tmpa', ti)], writes=[('bank', bu), ('mid', m, half)])
        kgroups = [(0, 16), (16, 16), (32, 11)]
        for m in range(KC):
            sis = [self.load_unit(wd[l, k0 * 128:(k0 + nk) * 128, m * 128:(m + 1) * 128], nk)
                   for (k0, nk) in kgroups]
            for half in range(2):
                cs = slice(half * HALF, (half + 1) * HALF)
                bi = self.bank()
                pb = self.banks[bi]
                fns = []
                for gi, (k0, nk) in enumerate(kgroups):
                    sl = self.slots[sis[gi]]
                    for kk in range(nk):
                        fns.append(lambda kk=kk, k0=k0, sl=sl, pb=pb, cs=cs: nc.tensor.matmul(
                            pb[:, 0:HALF], sl[:, kk, :], self.mid[:, k0 + kk, cs],
                            start=(k0 + kk == 0), stop=(k0 + kk == MFF - 1)))
                S.group('pe', fns,
                        reads=[('slot', s) for s in sis] + [('mid', mm, half) for mm in range(MFF)],
                        writes=[('bank', bi)])
                S.op('act', lambda pb=pb, m=m, cs=cs: nc.scalar.copy(
                    out=self.abuf[:, m, cs], in_=pb[:, 0:HALF]),
                    writes=[('bank', bi), ('abuf', m, half)])
        self.post_norm(l, j, 0.5)

    def prologue(self):
        nc, S = self.nc, self.S
        V, G, AC = nc.vector, nc.gpsimd, nc.scalar
        I32 = mybir.dt.int32
        RA, RM, RH, RX = self.RA, self.RM, self.RH, self.RX
        for R in (RA, RM, RH, RX):
            R.reset()
        PRO = ['PRO']

        def dv(fn):
            S.op('dve', fn, reads=PRO, writes=PRO)

        def ac(fn):
            S.op('act', fn, reads=PRO, writes=PRO)

        def tt(o, a, b, op):
            dv(lambda: V.tensor_tensor(out=o, in0=a, in1=b, op=op))

        def ts1(o, a, sc, op):
            dv(lambda: V.tensor_single_scalar(out=o, in_=a, scalar=sc, op=op))

        def stt(o, a, sc, b, op0, op1):
            dv(lambda: V.scalar_tensor_tensor(out=o, in0=a, scalar=sc, in1=b, op0=op0, op1=op1))

        CTre = RA.alloc([32, 128], F32)
        CTim = RA.alloc([32, 128], F32)
        CT = [CTre, CTim]
        io = RX.alloc([TT], I32)
        S.op('pool', lambda: G.iota(io.rearrange("p (c k) -> p c k", k=8), pattern=[[0, NT], [1, 8]],
                                    base=0, channel_multiplier=0), reads=PRO, writes=PRO)
        dv(lambda: V.tensor_copy(out=self.mask8[:], in_=io))
        ts1(self.mask8[:], self.mask8[:], 1.0, ALU.min)
        S.op('pool', lambda: G.iota(io[:, 0:512].rearrange("p (c k) -> p c k", k=32), pattern=[[0, 16], [1, 32]],
                                    base=0, channel_multiplier=0), reads=PRO, writes=PRO)
        dv(lambda: V.tensor_copy(out=self.mask32[:, TS:TT], in_=io[:, 0:512]))
        ts1(self.mask32[:, TS:TT], self.mask32[:, TS:TT], 1.0, ALU.min)
        dv(lambda: V.tensor_copy(out=self.mask32[:, 0:TS], in_=self.mask8[:, 0:TS]))
        S.op('pool', lambda: G.iota(io[0:32, 0:32], pattern=[[1, 32]], base=0, channel_multiplier=-1),
             reads=PRO, writes=PRO)
        dv(lambda: V.tensor_copy(out=self.maskc[:], in_=io[0:32, 0:32]))
        ts1(self.maskc[:], self.maskc[:], 0.0, ALU.is_ge)
        maskI = RH.alloc([4, 8], F32)
        S.op('pool', lambda: G.memset(maskI, 0.0), reads=PRO, writes=PRO)
        for i in range(4):
            for two in range(2):
                S.op('pool', lambda i=i, two=two: G.memset(
                    maskI[two * 64:(two + 1) * 64, i, 2 * i + two:2 * i + two + 1], 1.0), reads=PRO, writes=PRO)
        mask2 = RH.alloc([8], F32)
        m2b = RH.alloc([8], F32)
        S.op('pool', lambda: G.iota(io[:, 0:8], pattern=[[-16, 8]], base=0, channel_multiplier=1),
             reads=PRO, writes=PRO)
        dv(lambda: V.tensor_copy(out=mask2, in_=io[:, 0:8]))
        ts1(m2b, mask2, 16.0, ALU.is_lt)
        ts1(mask2, mask2, 0.0, ALU.is_ge)
        tt(mask2, mask2, m2b, ALU.mult)
        S.op('pool', lambda: G.memset(self.s5carry[:], 0.0), writes=['s5carry'])

        self.rowst = [RX.alloc([128], F32) for _ in range(2)]
        self.rowst_rr = 0

        def r2c(rows, n, dst):
            si = self.rowst_rr % 2
            self.rowst_rr += 1
            st = self.rowst[si]
            S.dma('sp', st[0:n, :], rows, reads=PRO, writes=[('rowst', si)])
            bi = self.bank()
            pb = self.banks[bi]
            S.group('pe', [lambda: nc.tensor.transpose(pb[:, 0:n], st[0:n, :], self.ident[0:n, 0:n])],
                    reads=[('rowst', si), 'ident'], writes=[('bank', bi)])
            S.op('act', lambda: AC.copy(out=dst, in_=pb[:, 0:n]), reads=PRO, writes=[('bank', bi)] + PRO)

        for j, nm in enumerate(["s5_d", "s5_b_glu", "hgrn_norm", "hgrn_lb"]):
            r2c(self.w[nm].rearrange("l (c p) -> (l c) p", p=128), 32, self.fvec[:, j, :])
        e = RH.alloc([32], F32)
        ssum = RH.alloc([8], F32)
        ac(lambda: AC.activation(out=e, in_=self.fvec[:, 3, :], func=AF.Exp))
        tt(ssum, e[:, 0:8], e[:, 8:16], ALU.add)
        tt(ssum, ssum, e[:, 16:24], ALU.add)
        tt(ssum, ssum, e[:, 24:32], ALU.add)
        dv(lambda: V.reciprocal(ssum, ssum))
        lbv = self.fvec[:, 3, :]
        dv(lambda: V.memset(lbv[:, 0:8], 0.0))
        tt(lbv[:, 8:16], e[:, 8:16], ssum, ALU.mult)
        tt(e[:, 16:24], e[:, 16:24], ssum, ALU.mult)
        tt(lbv[:, 16:24], lbv[:, 8:16], e[:, 16:24], ALU.add)
        tt(e[:, 24:32], e[:, 24:32], ssum, ALU.mult)
        tt(lbv[:, 24:32], lbv[:, 16:24], e[:, 24:32], ALU.add)
        dv(lambda: V.tensor_scalar(out=self.fvec[:, 4, :], in0=lbv, scalar1=-1.0, scalar2=1.0,
                                   op0=ALU.mult, op1=ALU.add))

        def T(n=32):
            return RH.alloc([n], F32)
        lr, li, ldt, dt_, mag, ang = T(), T(), T(), T(), T(), T()
        angs = RH.alloc([2, 32], F32)
        tq = RH.alloc([2, 32], F32)
        tf = RH.alloc([2, 32], F32)
        tiq = RH.alloc([2, 32], I32)
        msk = RH.alloc([2, 32], F32)
        scs = RH.alloc([2, 32], F32)
        Ar, Ai, pm1, den, zr, zi, Ir, Ii = T(), T(), T(), T(), T(), T(), T(), T()
        t1, t2, t3, t4 = T(), T(), T(), T()
        a2r, a2i, a4r, a4i = T(), T(), T(), T()
        Qr = RH.alloc([8, 32], F32)
        Qi = RH.alloc([8, 32], F32)
        Pr = RH.alloc([8, 32], F32)
        Pi = RH.alloc([8, 32], F32)
        ld2 = RH.alloc([2], F32)
        ldrows = RH.alloc([2, 64], F32)
        bd = [RH.alloc([4, 8, 16], F32) for _ in range(2)]
        cd = [RH.alloc([4, 2, 64], F32) for _ in range(2)]
        ust = {(kh, ri): RH.alloc([4, 4, 128], BF16) for kh in range(2) for ri in range(2)}
        ustC = ust
        Bst = [RM.alloc([32, 16], F32) for _ in range(2)]
        Cn = [RM.alloc([8, 64], F32) for _ in range(2)]
        Bt = [RM.alloc([8, 32, 16], F32) for _ in range(2)]
        RM_t1 = [RX.alloc([32, 16], F32) for _ in range(2)]
        ct1 = RX.alloc([4, 128], F32)
        ct2 = RX.alloc([4, 128], F32)

        def cmul(orr, oi, ar, ai, br, bi):
            tt(t1, ar, br, ALU.mult)
            tt(t2, ai, bi, ALU.mult)
            tt(t3, ar, bi, ALU.mult)
            tt(t4, ai, br, ALU.mult)
            tt(orr, t1, t2, ALU.subtract)
            tt(oi, t3, t4, ALU.add)

        for l in range(self.L):
            W = self.w
            r2c(W["s5_lam_re"][l].rearrange("(sc two) n -> sc (two n)", two=2), 32, lr)
            r2c(W["s5_lam_im"][l].rearrange("(sc two) n -> sc (two n)", two=2), 32, li)
            S.dma('sp', ld2[0:32, :], W["s5_log_dt"][l].rearrange("(sc two) -> sc two", two=2),
                  reads=PRO, writes=PRO)
            dv(lambda: V.tensor_copy(out=ldrows[0:32], in_=ld2[0:32, :].unsqueeze(2).broadcast_to([32, 2, 64])))
            bi = self.bank()
            pb = self.banks[bi]
            S.group('pe', [lambda pb=pb: nc.tensor.transpose(
                pb[:, 0:32], ldrows[0:32].rearrange("p a b -> p (a b)"), self.ident[0:32, 0:32])],
                reads=PRO + ['ident'], writes=[('bank', bi)])
            S.op('act', lambda pb=pb: AC.copy(out=ldt, in_=pb[:, 0:32]), reads=PRO, writes=[('bank', bi)] + PRO)
            ac(lambda: AC.activation(out=dt_, in_=ldt, func=AF.Exp))
            tt(t1, lr, dt_, ALU.mult)
            ac(lambda: AC.activation(out=mag, in_=t1, func=AF.Exp))
            tt(angs[:, 0, :], li, dt_, ALU.mult)
            ts1(angs[:, 1, :], angs[:, 0, :], math.pi / 2, ALU.add)
            ts1(tq, angs, 1.0 / TWO_PI, ALU.mult)
            dv(lambda: V.tensor_copy(out=tiq, in_=tq))
            dv(lambda: V.tensor_copy(out=tf, in_=tiq))
            stt(tq, tf, -TWO_PI, angs, ALU.mult, ALU.add)
            ts1(msk, tq, math.pi, ALU.is_gt)
            stt(tq, msk, -TWO_PI, tq, ALU.mult, ALU.add)
            ts1(msk, tq, -math.pi, ALU.is_lt)
            stt(tq, msk, TWO_PI, tq, ALU.mult, ALU.add)
            ac(lambda: AC.activation(out=scs, in_=tq, func=AF.Sin))
            tt(Ar, mag, scs[:, 1, :], ALU.mult)
            tt(Ai, mag, scs[:, 0, :], ALU.mult)
            ts1(pm1, Ar, -1.0, ALU.add)
            tt(t1, lr, lr, ALU.mult)
            tt(t2, li, li, ALU.mult)
            tt(den, t1, t2, ALU.add)
            dv(lambda: V.reciprocal(den, den))
            tt(t1, pm1, lr, ALU.mult)
            tt(t2, Ai, li, ALU.mult)
            tt(t1, t1, t2, ALU.add)
            tt(zr, t1, den, ALU.mult)
            tt(t1, Ai, lr, ALU.mult)
            tt(t2, pm1, li, ALU.mult)
            tt(t1, t1, t2, ALU.subtract)
            tt(zi, t1, den, ALU.mult)
            tt(t1, Ar, Ar, ALU.mult)
            tt(t2, Ai, Ai, ALU.mult)
            tt(t1, t1, t2, ALU.add)
            dv(lambda: V.reciprocal(t1, t1))
            tt(Ir, Ar, t1, ALU.mult)
            tt(Ii, Ai, t1, ALU.mult)
            ts1(Ii, Ii, -1.0, ALU.mult)
            dv(lambda: V.tensor_copy(out=Qr[:, 7, :], in_=zr))
            dv(lambda: V.tensor_copy(out=Qi[:, 7, :], in_=zi))
            dv(lambda: V.memset(Pr[:, 7, :], 1.0))
            dv(lambda: V.memset(Pi[:, 7, :], 0.0))
            for k in range(6, -1, -1):
                cmul(Qr[:, k, :], Qi[:, k, :], Qr[:, k + 1, :], Qi[:, k + 1, :], Ar, Ai)
                cmul(Pr[:, k, :], Pi[:, k, :], Pr[:, k + 1, :], Pi[:, k + 1, :], Ir, Ii)
            cmul(a2r, a2i, Ar, Ai, Ar, Ai)
            cmul(a4r, a4i, a2r, a2i, a2r, a2i)
            cmul(self.tabA[:, l, 0, :], self.tabA[:, l, 3, :], a4r, a4i, a4r, a4i)
            dv(lambda l=l: V.tensor_copy(out=self.tabA[:, l, 1, :], in_=self.tabA[:, l, 0, :]))
            ts1(self.tabA[:, l, 2, :], self.tabA[:, l, 3, :], -1.0, ALU.mult)
            with nc.allow_non_contiguous_dma(reason="64B runs"):
                S.dma('sp', Bst[0], W["s5_b_re"][l].rearrange("(sc two) n c -> (two n) sc c", two=2),
                      reads=PRO, writes=PRO)
                S.dma('sp', Bst[1], W["s5_b_im"][l].rearrange("(sc two) n c -> (two n) sc c", two=2),
                      reads=PRO, writes=PRO)
                S.dma('sp', Cn[0], W["s5_c_re"][l].rearrange("(uc g8) c n -> (g8 c) uc n", g8=8),
                      reads=PRO, writes=PRO)
                S.dma('sp', Cn[1], W["s5_c_im"][l].rearrange("(uc g8) c n -> (g8 c) uc n", g8=8),
                      reads=PRO, writes=PRO)
            for k in range(8):
                qr = Qr[:, k, :].unsqueeze(2).broadcast_to([128, 32, 16])
                qi = Qi[:, k, :].unsqueeze(2).broadcast_to([128, 32, 16])
                o1 = Bt[0][:, k]
                o2 = Bt[1][:, k]
                tta = RM_t1[0]
                ttb = RM_t1[1]
                tt(tta, qr, Bst[0], ALU.mult)
                tt(ttb, qi, Bst[1], ALU.mult)
                tt(o1, tta, ttb, ALU.subtract)
                tt(tta, qr, Bst[1], ALU.mult)
                tt(ttb, qi, Bst[0], ALU.mult)
                tt(o2, tta, ttb, ALU.add)
            mI = maskI.unsqueeze(3).broadcast_to([128, 4, 8, 16])
            n_bd = 0
            for uc in range(8):
                for k in range(8):
                    kh, kk = k // 4, k % 4
                    for ri in range(2):
                        b_i = n_bd % 2
                        n_bd += 1
                        bdt = bd[b_i]
                        src = Bt[ri][:, k, uc * 4:(uc + 1) * 4, :].unsqueeze(2).broadcast_to([128, 4, 8, 16])
                        S.op('dve', lambda bdt=bdt, src=src: V.tensor_tensor(out=bdt, in0=src, in1=mI, op=ALU.mult),
                             reads=PRO, writes=[('bd', b_i)])
                        bi = self.bank()
                        pb = self.banks[bi]
                        S.group('pe', [
                            (lambda i=i, pb=pb, bdt=bdt: nc.tensor.transpose(
                                pb[:, i * 128:(i + 1) * 128], bdt[:, i].rearrange("p a b -> p (a b)"),
                                self.ident[:, :])) for i in range(4)],
                            reads=[('bd', b_i), 'ident'], writes=[('bank', bi)])
                        u = ust[(kh, ri)]
                        S.op('act', lambda pb=pb, u=u, kk=kk: AC.copy(
                            out=u[:, :, kk, :], in_=pb[:, 0:512].rearrange("p (i s) -> p i s", s=128)),
                            writes=[('bank', bi), ('ust', kh, ri)])
                for kh in range(2):
                    for ri in range(2):
                        S.dma('sp', self.s5bt[l, uc, kh, ri], ust[(kh, ri)].rearrange("p a b c -> p (a b c)"),
                              reads=[('ust', kh, ri)], writes=['s5scr'])
            m2v = mask2.rearrange("p (i two) -> p i two", two=2).unsqueeze(3).broadcast_to([128, 4, 2, 64])
            n_cd = 0
            for uc in range(8):
                for ri in range(2):
                    c_i = n_cd % 2
                    n_cd += 1
                    cdt = cd[c_i]
                    src = Cn[ri][:, uc, :].unsqueeze(1).unsqueeze(1).broadcast_to([128, 4, 2, 64])
                    S.op('dve', lambda cdt=cdt, src=src: V.tensor_tensor(out=cdt, in0=src, in1=m2v, op=ALU.mult),
                         reads=PRO, writes=[('cd', c_i)])
                    bi = self.bank()
                    pb = self.banks[bi]
                    S.group('pe', [
                        (lambda i=i, pb=pb, cdt=cdt: nc.tensor.transpose(
                            pb[:, i * 128:(i + 1) * 128], cdt[:, i].rearrange("p a b -> p (a b)"),
                            self.ident[:, :])) for i in range(4)],
                        reads=[('cd', c_i), 'ident'], writes=[('bank', bi)])
                    S.op('act', lambda pb=pb, uc=uc, ri=ri: AC.copy(
                        out=CT[ri][:, uc * 4:(uc + 1) * 4, :], in_=pb[:, 0:512].rearrange("p (i s) -> p i s", s=128)),
                        reads=PRO, writes=[('bank', bi)] + PRO)
            for uc in range(8):
                sl4 = slice(uc * 4, (uc + 1) * 4)
                for k in range(8):
                    kh, kk = k // 4, k % 4
                    pr = Pr[:, k, sl4].unsqueeze(2).broadcast_to([128, 4, 128])
                    pi = Pi[:, k, sl4].unsqueeze(2).broadcast_to([128, 4, 128])
                    tt(ct1, pr, CT[0][:, sl4, :], ALU.mult)
                    tt(ct2, pi, CT[1][:, sl4, :], ALU.mult)
                    S.op('dve', lambda kh=kh, kk=kk: V.tensor_tensor(
                        out=ustC[(kh, 0)][:, :, kk, :], in0=ct1, in1=ct2, op=ALU.subtract),
                        reads=PRO, writes=PRO + [('ust', kh, 0)])
                    tt(ct1, pr, CT[1][:, sl4, :], ALU.mult)
                    tt(ct2, pi, CT[0][:, sl4, :], ALU.mult)
                    S.op('dve', lambda kh=kh, kk=kk: V.scalar_tensor_tensor(
                        out=ustC[(kh, 1)][:, :, kk, :], in0=ct1, scalar=-1.0, in1=ct2,
                        op0=ALU.mult, op1=ALU.subtract),
                        reads=PRO, writes=PRO + [('ust', kh, 1)])
                for kh in range(2):
                    for ri in range(2):
                        S.dma('sp', self.s5ct[l, uc, kh, ri], ustC[(kh, ri)].rearrange("p a b c -> p (a b c)"),
                              reads=[('ust', kh, ri)], writes=['s5scr'])
        S.barrier(('pe', 'act', 'dve', 'sp', 'pool'))

    def mixer(self, t, l):
        nc, S = self.nc, self.S
        V, AC = nc.vector, nc.scalar
        W = self.w
        RA, RM = self.RA, self.RM
        RA.reset()
        RM.reset()
        last_tile = (t == self.ntiles - 1)
        self.pre_norm(l, 1)
        xkeys = lambda half: [('xn', c, half) for c in range(KC)]
        xrhs = lambda kc, cs: self.xn[:, kc, cs]
        ymix = RM.alloc([16, TT], BF16)
        m_mark = RM.off
        ubf = RM.alloc([8, TT], BF16)
        ysb = RM.alloc([8, TT], BF16)
        xt = RM.alloc([2, 4, TT], BF16)
        Xse = RM.alloc([2, 4, 32], F32)
        XE = RA.alloc([2, 32, 69], F32)
        sT = [RA.alloc([2, 32], F32) for _ in range(2)]
        sU = [RA.alloc([2, 32], F32) for _ in range(2)]
        sS = [RA.alloc([4, 32], F32) for _ in range(2)]
        A8 = self.tabA[:, l]

        for uc in range(8):
            def ev(half, cs, pb, bi, uc=uc):
                S.op('dve', lambda: V.tensor_copy(out=ubf[:, uc, cs], in_=pb[:, 0:HALF]),
                     writes=[('bank', bi), ('ubf', uc, half)])
            self.proj(W["w_in"][l, :, uc * 128:(uc + 1) * 128], KC, xrhs, xkeys, ev)

        for ri, nm in enumerate(("s5re_in", "s5im_in")):
            si = self.stage_rr % 2
            self.stage_rr += 1
            st = self.sst[si]
            S.dma('sp', st[:, :], W[nm][l, t * TSQ:(t + 1) * TSQ].rearrange("b (sc two) n -> (b sc) (two n)", two=2),
                  writes=[('sst', si)])
            bi = self.bank()
            pb = self.banks[bi]
            S.group('pe', [lambda pb=pb, st=st: nc.tensor.transpose(pb[:, 0:128], st[:, :], self.ident[:, :])],
                    reads=[('sst', si), 'ident'], writes=[('bank', bi)])
            S.op('act', lambda pb=pb, ri=ri: AC.copy(
                out=XE[:, ri, :, 0:4], in_=pb[:, 0:128].rearrange("p (b s) -> p s b", s=32)),
                writes=[('bank', bi), 'XE'])
        S.op('dve', lambda: V.tensor_copy(out=XE[:, :, :, 4], in_=self.s5carry[:, l]),
             reads=['s5carry'], writes=['XE'])

        for uc in range(8):
            sl4 = slice(uc * 4, (uc + 1) * 4)
            un = {(kh, ri): self.load_s5unit(self.s5bt[l, uc, kh, ri]) for kh in range(2) for ri in range(2)}
            uv = ubf[:, uc, :].rearrange("p (c k) -> p c k", k=8)
            for ri in range(2):
                bi = self.bank()
                pb = self.banks[bi]
                fns = []
                for i in range(4):
                    for k in range(8):
                        sl = self.slots[un[(k // 4, ri)]]
                        fns.append(lambda i=i, k=k, sl=sl, pb=pb: nc.tensor.matmul(
                            pb[:, i * NT:(i + 1) * NT], sl[:, i * 4 + k % 4, :], uv[:, :, k],
                            start=(k == 0), stop=(k == 7)))
                S.group('pe', fns, reads=[('slot', un[(kh, ri)]) for kh in range(2)] +
                        [('ubf', uc, 0), ('ubf', uc, 1)], writes=[('bank', bi)])
                pv = pb[:, 0:4 * NT].rearrange("p (i c) -> p i c", c=NT)
                S.op('act', lambda pv=pv, ri=ri, sl4=sl4: AC.copy(out=XE[:, ri, sl4, 5:69], in_=pv[:, :, 4:NT]),
                     writes=[('bank', bi), 'XE'])
                S.op('act', lambda pv=pv, ri=ri, sl4=sl4: AC.copy(
                    out=Xse[:, ri, :, sl4], in_=pv[:, :, 0:4].rearrange("p i b -> p b i")),
                    writes=[('bank', bi), 'Xse'])

        def tt(o, a, b, op, r, w):
            S.op('dve', lambda: V.tensor_tensor(out=o, in0=a, in1=b, op=op), reads=r, writes=w)
        CA = A8[:, 0:2, :]
        for b in range(TSQ):
            ss = sS[b % 2]
            tt(ss[:, 0:2, :], CA, XE[:, :, :, b], ALU.mult, ['XE'], [('sS', b % 2)])
            tt(ss[:, 2, :], A8[:, 2, :], XE[:, 1, :, b], ALU.mult, ['XE'], [('sS', b % 2)])
            tt(ss[:, 3, :], A8[:, 3, :], XE[:, 0, :, b], ALU.mult, ['XE'], [('sS', b % 2)])
            tt(Xse[:, :, b, :], Xse[:, :, b, :], ss[:, 0:2, :], ALU.add, [('sS', b % 2), 'Xse'], ['Xse'])
            tt(Xse[:, :, b, :], Xse[:, :, b, :], ss[:, 2:4, :], ALU.add, [('sS', b % 2), 'Xse'], ['Xse'])
        for idx in range(4, 68):
            ss = sS[idx % 2]
            kx = ('sS', idx % 2)
            tt(ss[:, 0:2, :], CA, XE[:, :, :, idx], ALU.mult, ['XE'], [kx])
            tt(ss[:, 2, :], A8[:, 2, :], XE[:, 1, :, idx], ALU.mult, ['XE'], [kx])
            tt(ss[:, 3, :], A8[:, 3, :], XE[:, 0, :, idx], ALU.mult, ['XE'], [kx])
            tt(XE[:, :, :, idx + 1], XE[:, :, :, idx + 1], ss[:, 0:2, :], ALU.add, [kx, 'XE'], ['XE'])
            tt(XE[:, :, :, idx + 1], XE[:, :, :, idx + 1], ss[:, 2:4, :], ALU.add, [kx, 'XE'], ['XE'])
        S.op('dve', lambda: V.tensor_copy(out=self.s5carry[:, l], in_=XE[:, :, :, 68]),
             reads=['XE'], writes=['s5carry'])
        for ri in range(2):
            bi = self.bank()
            pb = self.banks[bi]
            S.group('pe', [lambda pb=pb, ri=ri: nc.tensor.transpose(
                pb[:, 0:128], Xse[:, ri].rearrange("p b s -> p (b s)"), self.ident[:, :])],
                reads=['Xse', 'ident'], writes=[('bank', bi)])
            si = self.stage_rr % 2
            self.stage_rr += 1
            st = self.sst[si]
            S.op('act', lambda pb=pb, st=st: AC.copy(out=st[:, :], in_=pb[:, 0:128]),
                 writes=[('bank', bi), ('sst', si)])
            ok = self.key('o_s5s')
            S.dma('sp', self.o_s5s[ri][l, t * TSQ:(t + 1) * TSQ].rearrange("b (sc two) n -> (b sc) (two n)", two=2),
                  st[:, :], reads=[('sst', si)], writes=[ok])
            self.out_keys.append(ok)
            if last_tile:
                bi = self.bank()
                pb = self.banks[bi]
                S.group('pe', [lambda pb=pb, ri=ri: nc.tensor.transpose(
                    pb[0:32, 0:128], XE[:, ri, :, 68], self.ident[:, :])],
                    reads=['XE', 'ident'], writes=[('bank', bi)])
                si = self.stage_rr % 2
                self.stage_rr += 1
                st = self.sst[si]
                S.op('act', lambda pb=pb, st=st: AC.copy(out=st[0:32, :], in_=pb[0:32, 0:128]),
                     writes=[('bank', bi), ('sst', si)])
                ok = self.key('o_s5p')
                S.dma('sp', self.o_s5p[ri][l].rearrange("(sc two) n -> sc (two n)", two=2), st[0:32, :],
                      reads=[('sst', si)], writes=[ok])
                self.out_keys.append(ok)

        for uc in range(8):
            un = {(kh, ri): self.load_s5unit(self.s5bt[l, uc, kh, ri]) for kh in range(2) for ri in range(2)}
            for i in range(4):
                sc = uc * 4 + i
                for ri in range(2):
                    for half in range(2):
                        cs = slice(half * HALF, (half + 1) * HALF)
                        bi = self.bank()
                        pb = self.banks[bi]
                        hv = pb[:, 0:HALF].rearrange("p (c k) -> p c k", k=8)
                        uv = ubf[:, uc, cs].rearrange("p (c k) -> p c k", k=8)
                        fns = [(lambda k=k, hv=hv, uv=uv: nc.tensor.matmul(
                            hv[:, :, k], self.slots[un[(k // 4, ri)]][:, i * 4 + k % 4, :], uv[:, :, k],
                            start=True, stop=True)) for k in range(8)]
                        S.group('pe', fns, reads=[('slot', un[(kh, ri)]) for kh in range(2)] + [('ubf', uc, half)],
                                writes=[('bank', bi)])
                        c0 = half * 34
                        pr_ = XE[:, 0, sc, c0:c0 + 34]
                        pi_ = XE[:, 1, sc, c0:c0 + 34]
                        if ri == 0:
                            ops = ((pr_, A8[:, 0, sc:sc + 1]), (pi_, A8[:, 2, sc:sc + 1]))
                        else:
                            ops = ((pi_, A8[:, 0, sc:sc + 1]), (pr_, A8[:, 3, sc:sc + 1]))
                        for (src, scal) in ops:
                            S.op('dve', lambda hv=hv, src=src, scal=scal: V.scalar_tensor_tensor(
                                out=hv[:, :, 0], in0=src, scalar=scal, in1=hv[:, :, 0],
                                op0=ALU.mult, op1=ALU.add), reads=['XE'], writes=[('bank', bi)])
                        S.op('dve', lambda pb=pb, ri=ri, i=i, cs=cs: V.tensor_tensor_scan(
                            out=xt[:, ri, i, cs], data0=self.mask8[:, cs], data1=pb[:, 0:HALF], initial=0.0,
                            op0=ALU.mult, op1=ALU.add), writes=[('bank', bi), ('xt', ri, i, half)])
            uc_ = {(kh, ri): self.load_s5unit(self.s5ct[l, uc, kh, ri]) for kh in range(2) for ri in range(2)}
            for half in range(2):
                cs = slice(half * HALF, (half + 1) * HALF)
                bi = self.bank()
                pb = self.banks[bi]
                hv = pb[:, 0:HALF].rearrange("p (c k) -> p c k", k=8)
                fns = []
                for k in range(8):
                    for i in range(4):
                        for ri in range(2):
                            xv = xt[:, ri, i, cs].rearrange("p (c k) -> p c k", k=8)
                            fns.append(lambda k=k, i=i, ri=ri, xv=xv, hv=hv: nc.tensor.matmul(
                                hv[:, :, k], self.slots[uc_[(k // 4, ri)]][:, i * 4 + k % 4, :], xv[:, :, k],
                                start=(i == 0 and ri == 0), stop=(i == 3 and ri == 1)))
                S.group('pe', fns, reads=[('slot', s_) for s_ in uc_.values()] +
                        [('xt', ri, i, half) for ri in range(2) for i in range(4)], writes=[('bank', bi)])
                t1i, t1 = self.tmp()
                t2i, t2 = self.tmp()
                dcol = self.fvec[:, 0, l * 8 + uc:l * 8 + uc + 1]
                S.op('dve', lambda t1=t1, pb=pb, cs=cs, dcol=dcol, uc=uc: V.scalar_tensor_tensor(
                    out=t1[:], in0=ubf[:, uc, cs], scalar=dcol, in1=pb[:, 0:HALF], op0=ALU.mult, op1=ALU.add),
                    reads=[('ubf', uc, half)], writes=[('bank', bi), ('tmpa', t1i)])
                S.op('dve', lambda t1=t1, t2=t2: V.tensor_tensor(out=t2[:], in0=t1[:], in1=t1[:], op=ALU.mult),
                     reads=[('tmpa', t1i)], writes=[('tmpa', t2i)])
                S.op('dve', lambda t2=t2: V.tensor_scalar(out=t2[:], in0=t2[:], scalar1=0.044715, scalar2=1.0,
                                                        op0=ALU.mult, op1=ALU.add),
                     reads=[('tmpa', t2i)], writes=[('tmpa', t2i)])
                S.op('dve', lambda t1=t1, t2=t2: V.tensor_tensor(out=t2[:], in0=t2[:], in1=t1[:], op=ALU.mult),
                     reads=[('tmpa', t1i), ('tmpa', t2i)], writes=[('tmpa', t2i)])
                S.op('act', lambda t2=t2: AC.activation(out=t2[:], in_=t2[:], func=AF.Sigmoid, scale=1.5957691216),
                     reads=[('tmpa', t2i)], writes=[('tmpa', t2i)])
                S.op('dve', lambda t1=t1, t2=t2, uc=uc, cs=cs: V.tensor_tensor(
                    out=ysb[:, uc, cs], in0=t1[:], in1=t2[:], op=ALU.mult),
                    reads=[('tmpa', t1i), ('tmpa', t2i)], writes=[('ysb', uc, half)])

        for m in range(8):
            def ev(half, cs, pb, bi, m=m):
                ti, tm = self.tmp()
                S.op('act', lambda: AC.activation(out=tm[:], in_=pb[:, 0:HALF], func=AF.Sigmoid,
                                                  bias=self.fvec[:, 1, l * 8 + m:l * 8 + m + 1]),
                     writes=[('bank', bi), ('tmpa', ti)])
                S.op('dve', lambda: V.tensor_tensor(out=ymix[:, m, cs], in0=ysb[:, m, cs], in1=tm[:], op=ALU.mult),
                     reads=[('tmpa', ti), ('ysb', m, half)], writes=[('ymix', m, half)])
            self.proj(W["s5_w_glu"][l, :, m * 128:(m + 1) * 128], 8, lambda kc, cs: ysb[:, kc, cs],
                      lambda half: [('ysb', c, half) for c in range(8)], ev)

        S.barrier()
        RA.reset()
        RM.reset(m_mark)
        self.hgrn(t, l, ymix, xrhs, xkeys)
        S.barrier()
        for m in range(KC):
            def ev(half, cs, pb, bi, m=m):
                S.op('act', lambda: AC.copy(out=self.abuf[:, m, cs], in_=pb[:, 0:HALF]),
                     writes=[('bank', bi), ('abuf', m, half)])
            self.proj(W["w_out"][l, :, m * 128:(m + 1) * 128], KC, lambda kc, cs: ymix[:, kc, cs],
                      lambda half: [('ymix', c, hf) for c in range(KC) for hf in range(2)], ev)
        self.post_norm(l, 1, 1.0)

    def hgrn(self, t, l, ymix, xrhs, xkeys):
        nc, S = self.nc, self.S
        V, AC = nc.vector, nc.scalar
        W = self.w
        RA, RM = self.RA, self.RM
        last_tile = (t == self.ntiles - 1)
        NH = 2
        F = lambda: RA.alloc([TT], F32)
        hb = [dict(sgf=F(), bb=F(), bb2=F(), eb=F(), enb=F(), qs=F(), vf=F(), gs=F()) for _ in range(NH)]
        hg1 = RM.alloc([TT], F32)
        rso = RM.alloc([TT], F32)
        Sall = RM.alloc([8, 128], F32)
        for s_ in range(NH):
            d = hb[s_]
            d['Ss'] = RM.alloc([TSQ, 128], F32)
            d['kt'] = RM.alloc([TT], BF16)
            d['qt'] = RM.alloc([TT], BF16)
            d['sqo'] = RM.alloc([TT], BF16)
            d['Sbf'] = [RM.alloc([128], BF16) for _ in range(3)]
            d['att'] = [RM.alloc([32], BF16) for _ in range(3)]
            d['kv'] = [RM.alloc([256], BF16) for _ in range(3)]
            d['oA'], d['oAk'] = self.banks[4 + 2 * s_], 4 + 2 * s_
            d['oB'], d['oBk'] = self.banks[5 + 2 * s_], 5 + 2 * s_
        if t == 0:
            S.op('dve', lambda: V.memset(Sall, 0.0), writes=['Sall'])
        else:
            S.dma('sp', Sall.rearrange("p a b -> p (a b)"), self.hgscr[l], reads=[('hgscr', l)], writes=['Sall'])
        self.nbanks_rot = 4
        for hp in range(8 // NH):
            for s_ in range(NH):
                d = hb[s_]
                hd = hp * NH + s_
                col = l * 8 + hd
                base = 1024 + hd * 128
                K_ = lambda nm, *a: (nm, s_) + a
                for (off, func, dst, nm) in ((1024, AF.Sigmoid, d['sgf'], 'sgf'), (0, AF.Silu, d['qs'], 'qs'),
                                              (2048, None, d['vf'], 'vf'), (3072, AF.Silu, d['gs'], 'gs')):
                    def ev(half, cs, pb, bi, func=func, dst=dst, nm=nm, s_=s_):
                        if func is None:
                            S.op('act', lambda: AC.copy(out=dst[:, cs], in_=pb[:, 0:HALF]),
                                 writes=[('bank', bi), (nm, s_, half)])
                        else:
                            S.op('act', lambda: AC.activation(out=dst[:, cs], in_=pb[:, 0:HALF], func=func),
                                 writes=[('bank', bi), (nm, s_, half)])
                    self.proj(W["w_in"][l, :, base + off:base + off + 128], KC, xrhs, xkeys, ev)
                both = lambda nm: [(nm, s_, 0), (nm, s_, 1)]
                sgf, bb, bb2, eb, enb, qs, kt, qt = (d['sgf'], d['bb'], d['bb2'], d['eb'], d['enb'], d['qs'],
                                                     d['kt'], d['qt'])
                S.op('dve', lambda: V.tensor_scalar(
                    out=sgf, in0=sgf, scalar1=self.fvec[:, 4, col:col + 1], scalar2=self.fvec[:, 3, col:col + 1],
                    op0=ALU.mult, op1=ALU.add), reads=both('sgf'), writes=both('sgf'))
                S.op('act', lambda: AC.activation(out=bb, in_=sgf, func=AF.Ln), reads=both('sgf'), writes=[K_('bb')])
                S.op('dve', lambda: V.tensor_tensor_scan(out=bb2, data0=self.mask32[:], data1=bb, initial=0.0,
                                                         op0=ALU.mult, op1=ALU.add), reads=[K_('bb')], writes=[K_('bb2')])
                S.op('act', lambda: AC.activation(out=eb, in_=bb2, func=AF.Exp), reads=[K_('bb2')], writes=[K_('eb')])
                S.op('act', lambda: AC.activation(out=enb, in_=bb2, func=AF.Exp, scale=-1.0),
                     reads=[K_('bb2')], writes=[K_('enb')])
                S.op('dve', lambda: V.tensor_scalar(out=sgf, in0=sgf, scalar1=-1.0, scalar2=1.0,
                                                    op0=ALU.mult, op1=ALU.add), reads=both('sgf'), writes=both('sgf'))
                S.op('dve', lambda: V.tensor_tensor(out=kt, in0=sgf, in1=enb, op=ALU.mult),
                     reads=both('sgf') + [K_('enb')], writes=[K_('kt')])
                S.op('dve', lambda: V.tensor_tensor(out=qt, in0=qs, in1=eb, op=ALU.mult),
                     reads=both('qs') + [K_('eb')], writes=[K_('qt')])
                b_s = bb2[:, 0:TS].rearrange("p (b k) -> p b k", k=8)
                b_p = bb2[:, TS:TT].rearrange("p (c k) -> p c k", k=32)
                S.op('dve', lambda: V.tensor_tensor(
                    out=enb[:, 0:TS].rearrange("p (b k) -> p b k", k=8),
                    in0=b_s[:, :, 7:8].broadcast_to([128, TSQ, 8]), in1=b_s, op=ALU.subtract),
                    reads=[K_('bb2')], writes=[K_('enb')])
                S.op('dve', lambda: V.tensor_tensor(
                    out=enb[:, TS:TT].rearrange("p (c k) -> p c k", k=32),
                    in0=b_p[:, :, 31:32].broadcast_to([128, 16, 32]), in1=b_p, op=ALU.subtract),
                    reads=[K_('bb2')], writes=[K_('enb')])
                S.op('act', lambda: AC.activation(out=enb, in_=enb, func=AF.Exp), reads=[K_('enb')], writes=[K_('enb')])
                S.op('dve', lambda: V.tensor_tensor(out=enb, in0=enb, in1=sgf, op=ALU.mult),
                     reads=[K_('enb')] + both('sgf'), writes=[K_('enb')])
                S.dma('sp', d['Ss'], W["hg_in"][l, t * TSQ:(t + 1) * TSQ, hd].rearrange("b k v -> k b v"),
                      writes=[K_('Ss')])
                d['chunks'] = [(b * 8, 8, d['Ss'][:, b, :], K_('Ss')) for b in range(TSQ)] + \
                              [(TS + j * 32, 32, Sall[:, hd, :], ('Sall', hd)) for j in range(16)]
            nchunk = TSQ + 16

            def stage1(s_, idx):
                d = hb[s_]
                c0, C, St, skey = d['chunks'][idx]
                cc = slice(c0, c0 + C)
                ci = idx % 3
                if C == 8 or idx == TSQ:
                    S.op('act', lambda: AC.copy(out=d['Sbf'][ci], in_=St), reads=[skey, 'Sall'],
                         writes=[('Sbf', s_, ci)])
                bi = self.bank()
                pb = self.banks[bi]
                S.group('pe', [
                    lambda: nc.tensor.matmul(pb[0:C, 0:C], d['kt'][:, cc], d['qt'][:, cc], start=True, stop=True),
                    lambda: nc.tensor.transpose(pb[0:C, 128:256], d['enb'][:, cc], self.ident[:, :]),
                    lambda: nc.tensor.transpose(pb[0:C, 256:384], d['vf'][:, cc], self.ident[:, :])],
                    reads=[('kt', s_), ('qt', s_), ('enb', s_), ('vf', s_, 0), ('vf', s_, 1), 'ident'],
                    writes=[('bank', bi)])
                S.op('dve', lambda: V.tensor_tensor(
                    out=d['att'][ci][0:C, 0:C], in0=pb[0:C, 0:C], in1=self.maskc[0:C, 0:C], op=ALU.mult),
                    reads=['maskc'], writes=[('bank', bi), ('att', s_, ci)])
                S.op('act', lambda: AC.copy(out=d['kv'][ci][0:C, :], in_=pb[0:C, 128:384]),
                     writes=[('bank', bi), ('kv', s_, ci)])

            def stage2(s_, idx):
                d = hb[s_]
                c0, C, St, skey = d['chunks'][idx]
                cc = slice(c0, c0 + C)
                ci = idx % 3
                if c0 < HG_SPLIT:
                    ob, obk, oc = d['oA'], d['oAk'], slice(c0, c0 + C)
                else:
                    ob, obk, oc = d['oB'], d['oBk'], slice(c0 - HG_SPLIT, c0 - HG_SPLIT + C)
                kv, att = d['kv'][ci], d['att'][ci]
                S.group('pe', [
                    lambda: nc.tensor.matmul(ob[:, oc], kv[0:C, 128:256], att[0:C, 0:C], start=True, stop=False),
                    lambda: nc.tensor.matmul(ob[:, oc], d['Sbf'][ci], d['qt'][:, cc], start=False, stop=True)],
                    reads=[('kv', s_, ci), ('att', s_, ci), ('Sbf', s_, ci), ('qt', s_)], writes=[('bank', obk)])
                bi2 = self.bank()
                pb2 = self.banks[bi2]
                S.group('pe', [lambda: nc.tensor.matmul(
                    pb2[:, 0:128], kv[0:C, 0:128], kv[0:C, 128:256], start=True, stop=True)],
                    reads=[('kv', s_, ci)], writes=[('bank', bi2)])
                S.op('dve', lambda: V.scalar_tensor_tensor(
                    out=St, in0=St, scalar=d['eb'][:, c0 + C - 1:c0 + C], in1=pb2[:, 0:128],
                    op0=ALU.mult, op1=ALU.add), reads=[('eb', s_), skey, 'Sall'], writes=[('bank', bi2), skey])
                if C == 32 and idx != nchunk - 1:
                    S.op('act', lambda: AC.copy(out=d['Sbf'][(idx + 1) % 3], in_=St), reads=[skey],
                         writes=[('Sbf', s_, (idx + 1) % 3)])

            for idx in range(nchunk + 1):
                for s_ in range(NH):
                    if idx < nchunk:
                        stage1(s_, idx)
                for s_ in range(NH):
                    if idx >= 1:
                        stage2(s_, idx - 1)

            for s_ in range(NH):
                d = hb[s_]
                hd = hp * NH + s_
                col = l * 8 + hd
                ok = self.key('o_hgs')
                S.dma('sp', self.o_hgs[l, t * TSQ:(t + 1) * TSQ, hd].rearrange("b k v -> k b v"), d['Ss'],
                      reads=[('Ss', s_)], writes=[ok])
                self.out_keys.append(ok)
                gcol = self.fvec[:, 2, col:col + 1]
                sqo, gs = d['sqo'], d['gs']
                for (ob, obk, c0, n) in ((d['oA'], d['oAk'], 0, HG_SPLIT), (d['oB'], d['oBk'], HG_SPLIT, TT - HG_SPLIT)):
                    cc = slice(c0, c0 + n)
                    S.op('act', lambda: AC.activation(out=sqo[:, cc], in_=ob[:, 0:n], func=AF.Square),
                         writes=[('bank', obk), ('sqo', s_)])
                    bi = self.bank()
                    pb = self.banks[bi]
                    S.group('pe', [lambda: nc.tensor.matmul(
                        pb[:, 0:n], self.ones_b[:], sqo[:, cc], start=True, stop=True)],
                        reads=[('sqo', s_), 'ones_b'], writes=[('bank', bi)])
                    S.op('act', lambda: AC.activation(
                        out=hg1[:, cc], in_=pb[:, 0:n], func=AF.Sqrt, bias=self.epsb[:], scale=1.0 / 128),
                        reads=['epsb'], writes=[('bank', bi), 'hg1'])
                    S.op('dve', lambda: V.reciprocal(rso[:, cc], hg1[:, cc]), reads=['hg1'], writes=['rso'])
                    S.op('dve', lambda: V.scalar_tensor_tensor(
                        out=hg1[:, cc], in0=ob[:, 0:n], scalar=gcol, in1=rso[:, cc], op0=ALU.mult, op1=ALU.mult),
                        reads=['rso'], writes=[('bank', obk), 'hg1'])
                    S.op('dve', lambda: V.tensor_tensor(
                        out=ymix[:, 8 + hd, cc], in0=hg1[:, cc], in1=gs[:, cc], op=ALU.mult),
                        reads=['hg1', ('gs', s_, 0), ('gs', s_, 1)],
                        writes=[('ymix', 8 + hd, 0), ('ymix', 8 + hd, 1)])
        self.nbanks_rot = 8
        allS = [('Sall', hd) for hd in range(8)] + ['Sall']
        if last_tile:
            ok = self.key('o_hgp')
            S.dma('sp', self.o_hgp[l].rearrange("h k v -> k h v"), Sall, reads=allS, writes=[ok])
            self.out_keys.append(ok)
        else:
            S.dma('sp', self.hgscr[l], Sall.rearrange("p a b -> p (a b)"), reads=allS, writes=[('hgscr', l)])


def build_program(cfg):
    p = Prog(cfg)
    nc = p.build()
    return p, nc


_WNAMES = ["norm_pre", "norm_post", "ffn1_w_gate", "ffn1_w_up", "ffn1_w_down", "ffn2_w_gate", "ffn2_w_up",
           "ffn2_w_down", "w_in", "w_out", "s5_w_glu", "s5_lam_re", "s5_lam_im", "s5_log_dt", "s5_b_re",
           "s5_b_im", "s5_c_re", "s5_c_im", "s5_d", "s5_b_glu", "hgrn_lb", "hgrn_norm"]


def make_in_maps(inputs, ncores=NCORES):
    f = lambda a: np.ascontiguousarray(np.asarray(a, dtype=np.float32))
    shared = {k: f(inputs[k]) for k in _WNAMES}
    xp, xs = f(inputs["x_prompt"]), f(inputs["x_sample"])
    sre, sim, shg = f(inputs["state_s5_re"]), f(inputs["state_s5_im"]), f(inputs["state_hgrn"])
    maps = []
    for c in range(ncores):
        m = dict(shared)
        m["xp"] = xp[c % 4]
        sl = slice(c * NSAMP, (c + 1) * NSAMP)
        m["xs"] = xs[sl].reshape(NSAMP * DSEQ, D)
        m["s5re_in"] = np.ascontiguousarray(sre[:, sl])
        m["s5im_in"] = np.ascontiguousarray(sim[:, sl])
        m["hg_in"] = np.ascontiguousarray(shg[:, sl])
        maps.append(m)
    return maps


def kernel(**inputs):
    p, nc = build_program({})
    maps = make_in_maps(inputs)
    res = run_bass_kernel_spmd(nc, maps, core_ids=list(range(NCORES)))
    r = res.results
    y_prompt = np.stack([r[c]["yp"] for c in range(4)], 0)
    y_sample = np.concatenate([r[c]["ys"].reshape(NSAMP, DSEQ, D) for c in range(NCORES)], 0)
    re_p = np.stack([r[c]["s5re_p"] for c in range(4)], 1)
    im_p = np.stack([r[c]["s5im_p"] for c in range(4)], 1)
    hg_p = np.stack([r[c]["hg_p"] for c in range(4)], 1)
    re_s = np.concatenate([r[c]["s5re_s"] for c in range(NCORES)], 1)
    im_s = np.concatenate([r[c]["s5im_s"] for c in range(NCORES)], 1)
    hg_s = np.concatenate([r[c]["hg_s"] for c in range(NCORES)], 1)
    return tuple(np.ascontiguousarray(a, dtype=np.float32) for a in
                 (y_prompt, y_sample, re_p, im_p, hg_p, re_s, im_s, hg_s))
```

```python
import contextlib
import numpy as np
import concourse.bass as bass
import concourse.mybir as mybir
from concourse.bass_utils import run_bass_kernel_spmd

F32 = mybir.dt.float32
BF16 = mybir.dt.bfloat16
AF = mybir.ActivationFunctionType
ALU = mybir.AluOpType

D = 2048
KC = 16
DFF = 5504
MFF = 43
DEPTH = 4
NCORES = 8
SEQ = 2048
NSAMP = 16
DSEQ = 8
TP = 512
TSQ = 4
TS = TSQ * DSEQ
TT = TP + TS
HALF = TT // 2
NTILES = SEQ // TP
EPS = 1e-6
NSLOT = 8


class Sched:
    EPOCH = 20000

    def __init__(self, nc, stack):
        self.nc = nc
        self.stack = stack
        self.engs = {'pe': nc.tensor, 'act': nc.scalar, 'dve': nc.vector,
                     'pool': nc.gpsimd, 'sp': nc.sync}
        self.cur = {}
        self.seen = {e: {} for e in self.engs}
        self.res = {}
        self.nsem = 0
        self.dma_pool = {}
        self.ninst = 0

    def _new_sem(self, name):
        s = self.stack.enter_context(self.nc.semaphore(name))
        self.nsem += 1
        return s

    def _eng_sem(self, e):
        c = self.cur.get(e)
        if c is None or c[2] >= self.EPOCH:
            ep = 0 if c is None else c[1][1] + 1
            c = [self._new_sem(f"s_{e}_{ep}"), (e, ep), 0]
            self.cur[e] = c
        return c

    def _wait(self, e, tickets):
        need = {}
        for t, raw in tickets:
            if t is None:
                continue
            key, sem, val = t
            if key[0] == e and (e == 'pe' or not raw):
                continue
            if self.seen[e].get(key, 0) >= val:
                continue
            if key not in need or need[key][1] < val:
                need[key] = (sem, val)
        for key, (sem, val) in need.items():
            self.engs[e].wait_ge(sem, val)
            self.seen[e][key] = val
            self.ninst += 1

    def _deps(self, reads, writes):
        deps = []
        for r in reads:
            st = self.res.get(r)
            if st:
                deps.append((st[0], True))
        for w in writes:
            st = self.res.get(w)
            if st:
                deps.append((st[0], False))
                deps.extend((x, False) for x in st[1])
        return deps

    def _commit(self, ticket, reads, writes):
        for r in reads:
            st = self.res.setdefault(r, [None, []])
            st[1].append(ticket)
        for w in writes:
            self.res[w] = [ticket, []]

    def op(self, e, fn, reads=(), writes=()):
        return self.group(e, [fn], reads, writes)

    def group(self, e, fns, reads=(), writes=()):
        self._wait(e, self._deps(reads, writes))
        c = self._eng_sem(e)
        for fn in fns[:-1]:
            fn()
        inst = fns[-1]()
        inst.then_inc(c[0], 1)
        self.ninst += len(fns)
        c[2] += 1
        t = (c[1], c[0], c[2])
        self._commit(t, reads, writes)
        return t

    def dma(self, e, out, in_, reads=(), writes=(), nsem=8, **kw):
        pool = self.dma_pool.get(e)
        if pool is None:
            pool = {'slots': [[self._new_sem(f"d_{e}_{i}"), ('dma', e, i), 0, None]
                              for i in range(nsem)], 'rr': 0}
            self.dma_pool[e] = pool
        slot = pool['slots'][pool['rr'] % len(pool['slots'])]
        pool['rr'] += 1
        deps = self._deps(reads, writes)
        if slot[3] is not None:
            deps.append((slot[3], True))
        self._wait(e, deps)
        inst = self.engs[e].dma_start(out=out, in_=in_, **kw)
        inst.then_inc(slot[0], 16)
        self.ninst += 1
        slot[2] += 16
        t = (slot[1], slot[0], slot[2])
        slot[3] = t
        self._commit(t, reads, writes)
        return t

    def barrier(self, engs=('pe', 'act', 'dve', 'sp')):
        ts = []
        for e2, c in self.cur.items():
            if c[2] > 0:
                ts.append(((c[1], c[0], c[2]), True))
        p = self.dma_pool.get('sp')
        if p:
            for sl in p['slots']:
                if sl[3] is not None:
                    ts.append((sl[3], True))
        for e in engs:
            self._wait(e, ts)

    def wait_all(self, e, keys):
        deps = []
        for k in keys:
            st = self.res.get(k)
            if st:
                deps.append((st[0], True))
                deps.extend((x, True) for x in st[1])
        self._wait(e, deps)


import math

NT = 68
HG_SPLIT = 288
HC = 64
NPC = TP // HC
TWO_PI = 2.0 * math.pi


def _sz(dt):
    return 4 if dt == F32 or dt == mybir.dt.int32 else 2


class Region:
    def __init__(self, t, nbytes):
        self.t, self.n, self.off = t, nbytes, 0

    def reset(self, off=0):
        self.off = off

    def alloc(self, free, dt, parts=128):
        n = 1
        for f in free:
            n *= f
        nb = n * _sz(dt)
        nb = (nb + 3) // 4 * 4
        assert self.off + nb <= self.n, (self.off, nb, self.n)
        ap = self.t[0:parts, self.off // 2:(self.off + nb) // 2]
        self.off += nb
        if dt != BF16:
            ap = ap.bitcast(dt)
        if len(free) == 2:
            ap = ap.rearrange("p (a b) -> p a b", b=free[1])
        elif len(free) == 3:
            ap = ap.rearrange("p (a b c) -> p a b c", b=free[1], c=free[2])
        elif len(free) == 4:
            ap = ap.rearrange("p (a b c d) -> p a b c d", b=free[1], c=free[2], d=free[3])
        return ap


class Prog:
    def __init__(self, cfg):
        self.cfg = cfg
        self.nc = bass.Bass("TRN2", target_bir_lowering=False)
        self.stack = contextlib.ExitStack()
        self.S = Sched(self.nc, self.stack)
        self.bank_rr = 0
        self.slot_rr = 0
        self.stage_rr = 0
        self.tmp_rr = 0
        self.uid = 0
        self.out_keys = []

    def din(self, name, shape, dt=F32):
        return self.nc.dram_tensor(name, list(shape), dt, kind="ExternalInput").ap()

    def dout(self, name, shape, dt=F32):
        return self.nc.dram_tensor(name, list(shape), dt, kind="ExternalOutput").ap()

    def dscr(self, name, shape, dt):
        return self.nc.dram_tensor(name, list(shape), dt, kind="Internal").ap()

    def sb(self, name, shape, dt):
        return self.stack.enter_context(self.nc.sbuf_tensor(name, list(shape), dt))

    def ps(self, name, shape, dt=F32):
        return self.stack.enter_context(self.nc.psum_tensor(name, list(shape), dt))

    nbanks_rot = 8

    def bank(self):
        i = self.bank_rr % self.nbanks_rot
        self.bank_rr += 1
        return i

    def tmp(self):
        i = self.tmp_rr % len(self.tmpa)
        self.tmp_rr += 1
        return i, self.tmpa[i]

    def key(self, s):
        self.uid += 1
        return (s, self.uid)

    def build(self):
        nc, S, cfg = self.nc, self.S, self.cfg
        L = cfg.get('layers', DEPTH)
        ntiles = cfg.get('ntiles', NTILES)
        self.L, self.ntiles = L, ntiles
        self.do_mixer = cfg.get('mixer', True)
        self.xp = self.din("xp", [SEQ, D])
        self.xs = self.din("xs", [NSAMP * DSEQ, D])
        self.norm_pre = self.din("norm_pre", [DEPTH, 3, D])
        self.norm_post = self.din("norm_post", [DEPTH, 3, D])
        self.w = {}
        for nm, shp in [("ffn1_w_gate", [DEPTH, D, DFF]), ("ffn1_w_up", [DEPTH, D, DFF]),
                        ("ffn1_w_down", [DEPTH, DFF, D]), ("ffn2_w_gate", [DEPTH, D, DFF]),
                        ("ffn2_w_up", [DEPTH, D, DFF]), ("ffn2_w_down", [DEPTH, DFF, D]),
                        ("w_in", [DEPTH, D, 5120]), ("w_out", [DEPTH, D, D]),
                        ("s5_w_glu", [DEPTH, 1024, 1024]),
                        ("s5_lam_re", [DEPTH, 64, 64]), ("s5_lam_im", [DEPTH, 64, 64]),
                        ("s5_log_dt", [DEPTH, 64]),
                        ("s5_b_re", [DEPTH, 64, 64, 16]), ("s5_b_im", [DEPTH, 64, 64, 16]),
                        ("s5_c_re", [DEPTH, 64, 16, 64]), ("s5_c_im", [DEPTH, 64, 16, 64]),
                        ("s5_d", [DEPTH, 1024]), ("s5_b_glu", [DEPTH, 1024]),
                        ("hgrn_lb", [DEPTH, 1024]), ("hgrn_norm", [DEPTH, 1024]),
                        ("s5re_in", [DEPTH, NSAMP, 64, 64]), ("s5im_in", [DEPTH, NSAMP, 64, 64]),
                        ("hg_in", [DEPTH, NSAMP, 8, 128, 128])]:
            self.w[nm] = self.din(nm, shp)
        self.yp = self.dout("yp", [SEQ, D])
        self.ys = self.dout("ys", [NSAMP * DSEQ, D])
        self.o_s5p = [self.dout("s5re_p", [DEPTH, 64, 64]), self.dout("s5im_p", [DEPTH, 64, 64])]
        self.o_hgp = self.dout("hg_p", [DEPTH, 8, 128, 128])
        self.o_s5s = [self.dout("s5re_s", [DEPTH, NSAMP, 64, 64]), self.dout("s5im_s", [DEPTH, NSAMP, 64, 64])]
        self.o_hgs = self.dout("hg_s", [DEPTH, NSAMP, 8, 128, 128])
        self.s5bt = self.dscr("s5bt", [DEPTH, 8, 2, 2, 128, 2048], BF16)
        self.s5ct = self.dscr("s5ct", [DEPTH, 8, 2, 2, 128, 2048], BF16)
        self.hgscr = self.dscr("hgscr", [DEPTH, 128, 1024], F32)

        HB, XB, AB, MB = KC * TT * 4, KC * TT * 2, KC * TT * 4, MFF * TT * 2
        self.Hb = self.sb("Hb", [128, HB // 2], BF16)
        self.Xb = self.sb("Xb", [128, XB // 2], BF16)
        self.Ab = self.sb("Ab", [128, AB // 2], BF16)
        self.Mb = self.sb("Mb", [128, MB // 2], BF16)
        self.RH, self.RX = Region(self.Hb, HB), Region(self.Xb, XB)
        self.RA, self.RM = Region(self.Ab, AB), Region(self.Mb, MB)
        self.h = self.Hb[:, :].bitcast(F32).rearrange("p (c t) -> p c t", t=TT)
        self.xn = self.Xb[:, :].rearrange("p (c t) -> p c t", t=TT)
        self.abuf = self.Ab[:, :].bitcast(F32).rearrange("p (c t) -> p c t", t=TT)
        self.mid = self.Mb[:, :].rearrange("p (c t) -> p c t", t=TT)
        self.stage = [self.Mb[:, i * 4096:(i + 1) * 4096].bitcast(F32) for i in range(2)]
        self.slots = [self.sb(f"slot{i}", [128, KC, 128], BF16) for i in range(NSLOT)]
        self.ident = self.sb("ident", [128, 128], F32)
        self.ones_f = self.sb("ones_f", [128, 128], F32)
        self.ones_b = self.sb("ones_b", [128, 128], BF16)
        self.epsb = self.sb("epsb", [128, 1], F32)
        self.gpre = self.sb("gpre", [128, DEPTH * 3 * KC], F32)
        self.gpost = self.sb("gpost", [128, DEPTH * 3 * KC], F32)
        self.tmpa = [self.sb(f"tmpa{i}", [128, HALF], F32) for i in range(4)]
        self.sq4 = [self.sb(f"sq4_{i}", [128, 4, HALF], BF16) for i in range(2)]
        self.sq_rr = 0
        self.rstd = self.sb("rstd", [128, TT], F32)
        self.banks = [self.ps(f"bank{i}", [128, 512], F32) for i in range(8)]
        self.tabA = self.sb("tabA", [128, DEPTH, 4, 32], F32)
        self.fvec = self.sb("fvec", [128, 5, 32], F32)
        self.s5carry = self.sb("s5carry", [128, DEPTH, 2, 32], F32)
        self.mask8 = self.sb("mask8", [128, TT], F32)
        self.mask32 = self.sb("mask32", [128, TT], F32)
        self.maskc = self.sb("maskc", [HC, HC], F32)
        self.sst = [self.sb(f"sst{i}", [128, 128], F32) for i in range(2)]

        self.init_consts()
        if self.do_mixer:
            self.prologue()
        S.barrier(('pe', 'act', 'dve', 'sp'))
        for t in range(ntiles):
            self.load_tile(t)
            for l in range(L):
                self.ffn(l, 0)
                if self.do_mixer:
                    S.barrier()
                    self.mixer(t, l)
                    S.barrier()
                if cfg.get('ffn2', True):
                    self.ffn(l, 2)
            S.barrier()
            self.store_tile(t)
            S.barrier()
        self.finish()
        return nc

    def init_consts(self):
        nc, S = self.nc, self.S
        S.op('pool', lambda: nc.gpsimd.memset(self.ones_f[:], 1.0), writes=['ones_f'])
        S.op('pool', lambda: nc.gpsimd.memset(self.ones_b[:], 1.0), writes=['ones_b'])
        S.op('pool', lambda: nc.gpsimd.memset(self.epsb[:], EPS), writes=['epsb'])
        S.op('pool', lambda: nc.gpsimd.affine_select(
            out=self.ident[:], in_=self.ones_f[:], pattern=[[-1, 128]],
            compare_op=ALU.is_equal, fill=0.0, base=0, channel_multiplier=1),
            reads=['ones_f'], writes=['ident'])
        for (src, dst, key) in ((self.norm_pre, self.gpre, 'gpre'), (self.norm_post, self.gpost, 'gpost')):
            rows = src.rearrange("l j (c p) -> (l j c) p", p=128)
            for hf in range(2):
                self.rows_to_cols(rows[hf * 96:(hf + 1) * 96, :], 96, dst[:, hf * 96:(hf + 1) * 96], (key, hf))

    def rows_to_cols(self, rows, n, dst, wkey, eng='act'):
        nc, S = self.nc, self.S
        si = self.stage_rr % 2
        self.stage_rr += 1
        st = self.stage[si]
        S.dma('sp', st[0:n, 0:128], rows, writes=[('stage', si)])
        bi = self.bank()
        pb = self.banks[bi]
        S.group('pe', [lambda: nc.tensor.transpose(pb[:, 0:n], st[0:n, 0:128], self.ident[0:n, 0:n])],
                reads=[('stage', si), 'ident'], writes=[('bank', bi)])
        S.op('act', lambda: nc.scalar.copy(out=dst, in_=pb[:, 0:n]), writes=[('bank', bi), wkey])

    def hkeys(self, cs_list):
        return [('h', c, hf) for c in cs_list for hf in range(2)]

    def tile_blocks(self, t, pa, sa):
        blocks = [(sa[t * TS:(t + 1) * TS, :], TS, 0)]
        blocks += [(pa[t * TP + b * 128: t * TP + (b + 1) * 128, :], 128, TS + b * 128) for b in range(4)]
        return blocks

    def load_tile(self, t):
        nc, S = self.nc, self.S
        for src, n, col0 in self.tile_blocks(t, self.xp, self.xs):
            si = self.stage_rr % 2
            self.stage_rr += 1
            st = self.stage[si]
            S.dma('sp', st[0:n, :], src, writes=[('stage', si)])
            for c4 in range(KC // 4):
                bi = self.bank()
                pb = self.banks[bi]
                S.group('pe', [
                    (lambda cc=c4 * 4 + j, j=j, pb=pb, st=st, n=n: nc.tensor.transpose(
                        pb[:, j * 128: j * 128 + n], st[0:n, cc * 128:(cc + 1) * 128],
                        self.ident[0:n, 0:n]))
                    for j in range(4)], reads=[('stage', si), 'ident'], writes=[('bank', bi)])
                S.op('act', lambda pb=pb, c4=c4, col0=col0, n=n: nc.scalar.copy(
                    out=self.h[:, c4 * 4:(c4 + 1) * 4, col0:col0 + n],
                    in_=pb[:, 0:512].rearrange("p (j n) -> p j n", n=128)[:, :, 0:n]),
                    writes=[('bank', bi)] + self.hkeys(range(c4 * 4, c4 * 4 + 4)))

    def store_tile(self, t):
        nc, S = self.nc, self.S
        for bidx, (dst, n, col0) in enumerate(self.tile_blocks(t, self.yp, self.ys)):
            si = self.stage_rr % 2
            self.stage_rr += 1
            st = self.stage[si]
            for c4 in range(KC // 4):
                bi = self.bank()
                pb = self.banks[bi]
                S.group('pe', [
                    (lambda cc=c4 * 4 + j, j=j, pb=pb, n=n, col0=col0: nc.tensor.transpose(
                        pb[0:n, j * 128:(j + 1) * 128], self.h[:, cc, col0:col0 + n],
                        self.ident[:, :]))
                    for j in range(4)],
                    reads=self.hkeys(range(c4 * 4, c4 * 4 + 4)) + ['ident'], writes=[('bank', bi)])
                S.op('act', lambda pb=pb, st=st, c4=c4, n=n: nc.scalar.copy(
                    out=st[0:n, c4 * 512:(c4 + 1) * 512], in_=pb[0:n, 0:512]),
                    writes=[('bank', bi), ('stage', si)])
            ok = ('out', t, bidx)
            S.dma('sp', dst, st[0:n, :], reads=[('stage', si)], writes=[ok])
            self.out_keys.append(ok)

    def finish(self):
        self.S.wait_all('sp', self.out_keys)

    def load_unit(self, src_rows_cols, nk):
        nc, S = self.nc, self.S
        si = self.slot_rr % NSLOT
        self.slot_rr += 1
        sl = self.slots[si]
        S.dma('pool', sl[:, 0:nk, :], src_rows_cols.rearrange("(kc p) m -> p kc m", p=128),
              writes=[('slot', si)])
        return si

    def load_s5unit(self, src):
        S = self.S
        si = self.slot_rr % NSLOT
        self.slot_rr += 1
        sl = self.slots[si]
        S.dma('sp', sl[:, :, :], src.rearrange("p (a b) -> p a b", b=128),
              reads=['s5scr'], writes=[('slot', si)])
        return si

    def rms_stats(self, src, keyfn):
        nc, S = self.nc, self.S
        for half in range(2):
            cs = slice(half * HALF, (half + 1) * HALF)
            bi = self.bank()
            pb = self.banks[bi]
            for c4 in range(KC // 4):
                qi = self.sq_rr % 2
                self.sq_rr += 1
                sq = self.sq4[qi]
                S.op('act', lambda: nc.scalar.activation(
                    out=sq[:, :, :], in_=src[:, c4 * 4:(c4 + 1) * 4, cs], func=AF.Square),
                    reads=[keyfn(c, half) for c in range(c4 * 4, c4 * 4 + 4)], writes=[('sq4', qi)])
                S.group('pe', [
                    (lambda j=j: nc.tensor.matmul(
                        pb[:, 0:HALF], self.ones_b[:], sq[:, j, :],
                        start=(c4 == 0 and j == 0), stop=(c4 == KC // 4 - 1 and j == 3)))
                    for j in range(4)], reads=[('sq4', qi), 'ones_b'], writes=[('bank', bi)])
            ti, tm = self.tmp()
            S.op('act', lambda tm=tm, pb=pb: nc.scalar.activation(
                out=tm[:], in_=pb[:, 0:HALF], func=AF.Sqrt, bias=self.epsb[:], scale=1.0 / D),
                reads=['epsb'], writes=[('bank', bi), ('tmpa', ti)])
            S.op('dve', lambda tm=tm, cs=cs: nc.vector.reciprocal(self.rstd[:, cs], tm[:]),
                 reads=[('tmpa', ti)], writes=[('rstd', half)])

    def pre_norm(self, l, j):
        nc, S = self.nc, self.S
        self.rms_stats(self.h, lambda c, hf: ('h', c, hf))
        g = self.gpre
        gi = (l * 3 + j) * KC
        for half in range(2):
            cs = slice(half * HALF, (half + 1) * HALF)
            for c in range(KC):
                S.op('dve', lambda c=c, cs=cs: nc.vector.scalar_tensor_tensor(
                    out=self.xn[:, c, cs], in0=self.h[:, c, cs], scalar=g[:, gi + c:gi + c + 1],
                    in1=self.rstd[:, cs], op0=ALU.mult, op1=ALU.mult),
                    reads=[('h', c, half), ('gpre', 0), ('gpre', 1), ('rstd', half)],
                    writes=[('xn', c, half)])

    def post_norm(self, l, j, coef):
        nc, S = self.nc, self.S
        self.rms_stats(self.abuf, lambda c, hf: ('abuf', c, hf))
        g = self.gpost
        gi = (l * 3 + j) * KC
        for half in range(2):
            cs = slice(half * HALF, (half + 1) * HALF)
            for c in range(KC):
                ti, tm = self.tmp()
                S.op('dve', lambda c=c, tm=tm, cs=cs: nc.vector.scalar_tensor_tensor(
                    out=tm[:], in0=self.abuf[:, c, cs], scalar=g[:, gi + c:gi + c + 1],
                    in1=self.rstd[:, cs], op0=ALU.mult, op1=ALU.mult),
                    reads=[('abuf', c, half), ('gpost', 0), ('gpost', 1), ('rstd', half)],
                    writes=[('tmpa', ti)])
                S.op('dve', lambda c=c, tm=tm, cs=cs: nc.vector.scalar_tensor_tensor(
                    out=self.h[:, c, cs], in0=tm[:], scalar=float(coef),
                    in1=self.h[:, c, cs], op0=ALU.mult, op1=ALU.add),
                    reads=[('tmpa', ti)], writes=[('h', c, half)])

    def proj(self, wsrc, nk, rhs_fn, rkeys_fn, evac):
        nc, S = self.nc, self.S
        si = self.load_unit(wsrc, nk)
        sl = self.slots[si]
        for half in range(2):
            cs = slice(half * HALF, (half + 1) * HALF)
            bi = self.bank()
            pb = self.banks[bi]
            S.group('pe', [
                (lambda kc=kc, pb=pb, cs=cs: nc.tensor.matmul(
                    pb[:, 0:HALF], sl[:, kc, :], rhs_fn(kc, cs), start=(kc == 0), stop=(kc == nk - 1)))
                for kc in range(nk)],
                reads=[('slot', si)] + rkeys_fn(half), writes=[('bank', bi)])
            evac(half, cs, pb, bi)

    def ffn(self, l, j):
        nc, S = self.nc, self.S
        pfx = "ffn1" if j == 0 else "ffn2"
        wg, wu, wd = self.w[pfx + "_w_gate"], self.w[pfx + "_w_up"], self.w[pfx + "_w_down"]
        self.pre_norm(l, j)
        for m in range(MFF):
            sg = self.load_unit(wg[l, :, m * 128:(m + 1) * 128], KC)
            su = self.load_unit(wu[l, :, m * 128:(m + 1) * 128], KC)
            for half in range(2):
                cs = slice(half * HALF, (half + 1) * HALF)
                bg, bu = self.bank(), self.bank()
                pg, pu = self.banks[bg], self.banks[bu]
                for (si, pb, bi) in ((sg, pg, bg), (su, pu, bu)):
                    sl = self.slots[si]
                    S.group('pe', [
                        (lambda kc=kc, sl=sl, pb=pb, cs=cs: nc.tensor.matmul(
                            pb[:, 0:HALF], sl[:, kc, :], self.xn[:, kc, cs],
                            start=(kc == 0), stop=(kc == KC - 1)))
                        for kc in range(KC)],
                        reads=[('slot', si)] + [('xn', c, half) for c in range(KC)],
                        writes=[('bank', bi)])
                ti, tm = self.tmp()
                S.op('act', lambda tm=tm, pg=pg: nc.scalar.activation(
                    out=tm[:], in_=pg[:, 0:HALF], func=AF.Silu),
                    writes=[('bank', bg), ('tmpa', ti)])
                S.op('dve', lambda tm=tm, pu=pu, m=m, cs=cs: nc.vector.tensor_tensor(
                    out=self.mid[:, m, cs], in0=tm[:], in1=pu[:, 0:HALF], op=ALU.mult),
                    reads=[('tmpa', ti)], writes=[('bank', bu), ('mid', m, half)])
        kgroups = [(0, 16), (16, 16), (32, 11)]
        for m in range(KC):
            sis = [self.load_unit(wd[l, k0 * 128:(k0 + nk) * 128, m * 128:(m + 1) * 128], nk)
                   for (k0, nk) in kgroups]
            for half in range(2):
                cs = slice(half * HALF, (half + 1) * HALF)
                bi = self.bank()
                pb = self.banks[bi]
                fns = []
                for gi, (k0, nk) in enumerate(kgroups):
                    sl = self.slots[sis[gi]]
                    for kk in range(nk):
                        fns.append(lambda kk=kk, k0=k0, sl=sl, pb=pb, cs=cs: nc.tensor.matmul(
                            pb[:, 0:HALF], sl[:, kk, :], self.mid[:, k0 + kk, cs],
                            start=(k0 + kk == 0), stop=(k0 + kk == MFF - 1)))
                S.group('pe', fns,
                        reads=[('slot', s) for s in sis] + [('mid', mm, half) for mm in range(MFF)],
                        writes=[('bank', bi)])
                S.op('act', lambda pb=pb, m=m, cs=cs: nc.scalar.copy(
                    out=self.abuf[:, m, cs], in_=pb[:, 0:HALF]),
                    writes=[('bank', bi), ('abuf', m, half)])
        self.post_norm(l, j, 0.5)

    def prologue(self):
        nc, S = self.nc, self.S
        V, G, AC = nc.vector, nc.gpsimd, nc.scalar
        I32 = mybir.dt.int32
        RA, RM, RH, RX = self.RA, self.RM, self.RH, self.RX
        for R in (RA, RM, RH, RX):
            R.reset()
        PRO = ['PRO']

        def dv(fn):
            S.op('dve', fn, reads=PRO, writes=PRO)

        def ac(fn):
            S.op('act', fn, reads=PRO, writes=PRO)

        def tt(o, a, b, op):
            dv(lambda: V.tensor_tensor(out=o, in0=a, in1=b, op=op))

        def ts1(o, a, sc, op):
            dv(lambda: V.tensor_single_scalar(out=o, in_=a, scalar=sc, op=op))

        def stt(o, a, sc, b, op0, op1):
            dv(lambda: V.scalar_tensor_tensor(out=o, in0=a, scalar=sc, in1=b, op0=op0, op1=op1))

        CTre = RA.alloc([32, 128], F32)
        CTim = RA.alloc([32, 128], F32)
        CT = [CTre, CTim]
        io = RX.alloc([TT], I32)
        S.op('pool', lambda: G.iota(io.rearrange("p (c k) -> p c k", k=8), pattern=[[0, NT], [1, 8]],
                                    base=0, channel_multiplier=0), reads=PRO, writes=PRO)
        dv(lambda: V.tensor_copy(out=self.mask8[:], in_=io))
        ts1(self.mask8[:], self.mask8[:], 1.0, ALU.min)
        S.op('pool', lambda: G.iota(io[:, 0:512].rearrange("p (c k) -> p c k", k=HC), pattern=[[0, NPC], [1, HC]],
                                    base=0, channel_multiplier=0), reads=PRO, writes=PRO)
        dv(lambda: V.tensor_copy(out=self.mask32[:, TS:TT], in_=io[:, 0:512]))
        ts1(self.mask32[:, TS:TT], self.mask32[:, TS:TT], 1.0, ALU.min)
        dv(lambda: V.tensor_copy(out=self.mask32[:, 0:TS], in_=self.mask8[:, 0:TS]))
        S.op('pool', lambda: G.iota(io[0:HC, 0:HC], pattern=[[1, HC]], base=0, channel_multiplier=-1),
             reads=PRO, writes=PRO)
        dv(lambda: V.tensor_copy(out=self.maskc[:], in_=io[0:HC, 0:HC]))
        ts1(self.maskc[:], self.maskc[:], 0.0, ALU.is_ge)
        maskI = RH.alloc([4, 8], F32)
        S.op('pool', lambda: G.memset(maskI, 0.0), reads=PRO, writes=PRO)
        for i in range(4):
            for two in range(2):
                S.op('pool', lambda i=i, two=two: G.memset(
                    maskI[two * 64:(two + 1) * 64, i, 2 * i + two:2 * i + two + 1], 1.0), reads=PRO, writes=PRO)
        mask2 = RH.alloc([8], F32)
        m2b = RH.alloc([8], F32)
        S.op('pool', lambda: G.iota(io[:, 0:8], pattern=[[-16, 8]], base=0, channel_multiplier=1),
             reads=PRO, writes=PRO)
        dv(lambda: V.tensor_copy(out=mask2, in_=io[:, 0:8]))
        ts1(m2b, mask2, 16.0, ALU.is_lt)
        ts1(mask2, mask2, 0.0, ALU.is_ge)
        tt(mask2, mask2, m2b, ALU.mult)
        S.op('pool', lambda: G.memset(self.s5carry[:], 0.0), writes=['s5carry'])

        self.rowst = [RX.alloc([128], F32) for _ in range(2)]
        self.rowst_rr = 0

        def r2c(rows, n, dst):
            si = self.rowst_rr % 2
            self.rowst_rr += 1
            st = self.rowst[si]
            S.dma('sp', st[0:n, :], rows, reads=PRO, writes=[('rowst', si)])
            bi = self.bank()
            pb = self.banks[bi]
            S.group('pe', [lambda: nc.tensor.transpose(pb[:, 0:n], st[0:n, :], self.ident[0:n, 0:n])],
                    reads=[('rowst', si), 'ident'], writes=[('bank', bi)])
            S.op('act', lambda: AC.copy(out=dst, in_=pb[:, 0:n]), reads=PRO, writes=[('bank', bi)] + PRO)

        for j, nm in enumerate(["s5_d", "s5_b_glu", "hgrn_norm", "hgrn_lb"]):
            r2c(self.w[nm].rearrange("l (c p) -> (l c) p", p=128), 32, self.fvec[:, j, :])
        e = RH.alloc([32], F32)
        ssum = RH.alloc([8], F32)
        ac(lambda: AC.activation(out=e, in_=self.fvec[:, 3, :], func=AF.Exp))
        tt(ssum, e[:, 0:8], e[:, 8:16], ALU.add)
        tt(ssum, ssum, e[:, 16:24], ALU.add)
        tt(ssum, ssum, e[:, 24:32], ALU.add)
        dv(lambda: V.reciprocal(ssum, ssum))
        lbv = self.fvec[:, 3, :]
        dv(lambda: V.memset(lbv[:, 0:8], 0.0))
        tt(lbv[:, 8:16], e[:, 8:16], ssum, ALU.mult)
        tt(e[:, 16:24], e[:, 16:24], ssum, ALU.mult)
        tt(lbv[:, 16:24], lbv[:, 8:16], e[:, 16:24], ALU.add)
        tt(e[:, 24:32], e[:, 24:32], ssum, ALU.mult)
        tt(lbv[:, 24:32], lbv[:, 16:24], e[:, 24:32], ALU.add)
        dv(lambda: V.tensor_scalar(out=self.fvec[:, 4, :], in0=lbv, scalar1=-1.0, scalar2=1.0,
                                   op0=ALU.mult, op1=ALU.add))

        def T(n=32):
            return RH.alloc([n], F32)
        lr, li, ldt, dt_, mag, ang = T(), T(), T(), T(), T(), T()
        angs = RH.alloc([2, 32], F32)
        tq = RH.alloc([2, 32], F32)
        tf = RH.alloc([2, 32], F32)
        tiq = RH.alloc([2, 32], I32)
        msk = RH.alloc([2, 32], F32)
        scs = RH.alloc([2, 32], F32)
        Ar, Ai, pm1, den, zr, zi, Ir, Ii = T(), T(), T(), T(), T(), T(), T(), T()
        t1, t2, t3, t4 = T(), T(), T(), T()
        a2r, a2i, a4r, a4i = T(), T(), T(), T()
        Qr = RH.alloc([8, 32], F32)
        Qi = RH.alloc([8, 32], F32)
        Pr = RH.alloc([8, 32], F32)
        Pi = RH.alloc([8, 32], F32)
        ld2 = RH.alloc([2], F32)
        ldrows = RH.alloc([2, 64], F32)
        bd = [RH.alloc([4, 8, 16], F32) for _ in range(2)]
        cd = [RH.alloc([4, 2, 64], F32) for _ in range(2)]
        ust = {(kh, ri): RH.alloc([4, 4, 128], BF16) for kh in range(2) for ri in range(2)}
        ustC = ust
        Bst = [RM.alloc([32, 16], F32) for _ in range(2)]
        Cn = [RM.alloc([8, 64], F32) for _ in range(2)]
        Bt = [RM.alloc([8, 32, 16], F32) for _ in range(2)]
        RM_t1 = [RX.alloc([32, 16], F32) for _ in range(2)]
        ct1 = RX.alloc([4, 128], F32)
        ct2 = RX.alloc([4, 128], F32)

        def cmul(orr, oi, ar, ai, br, bi):
            tt(t1, ar, br, ALU.mult)
            tt(t2, ai, bi, ALU.mult)
            tt(t3, ar, bi, ALU.mult)
            tt(t4, ai, br, ALU.mult)
            tt(orr, t1, t2, ALU.subtract)
            tt(oi, t3, t4, ALU.add)

        for l in range(self.L):
            W = self.w
            r2c(W["s5_lam_re"][l].rearrange("(sc two) n -> sc (two n)", two=2), 32, lr)
            r2c(W["s5_lam_im"][l].rearrange("(sc two) n -> sc (two n)", two=2), 32, li)
            S.dma('sp', ld2[0:32, :], W["s5_log_dt"][l].rearrange("(sc two) -> sc two", two=2),
                  reads=PRO, writes=PRO)
            dv(lambda: V.tensor_copy(out=ldrows[0:32], in_=ld2[0:32, :].unsqueeze(2).broadcast_to([32, 2, 64])))
            bi = self.bank()
            pb = self.banks[bi]
            S.group('pe', [lambda pb=pb: nc.tensor.transpose(
                pb[:, 0:32], ldrows[0:32].rearrange("p a b -> p (a b)"), self.ident[0:32, 0:32])],
                reads=PRO + ['ident'], writes=[('bank', bi)])
            S.op('act', lambda pb=pb: AC.copy(out=ldt, in_=pb[:, 0:32]), reads=PRO, writes=[('bank', bi)] + PRO)
            ac(lambda: AC.activation(out=dt_, in_=ldt, func=AF.Exp))
            tt(t1, lr, dt_, ALU.mult)
            ac(lambda: AC.activation(out=mag, in_=t1, func=AF.Exp))
            tt(angs[:, 0, :], li, dt_, ALU.mult)
            ts1(angs[:, 1, :], angs[:, 0, :], math.pi / 2, ALU.add)
            ts1(tq, angs, 1.0 / TWO_PI, ALU.mult)
            dv(lambda: V.tensor_copy(out=tiq, in_=tq))
            dv(lambda: V.tensor_copy(out=tf, in_=tiq))
            stt(tq, tf, -TWO_PI, angs, ALU.mult, ALU.add)
            ts1(msk, tq, math.pi, ALU.is_gt)
            stt(tq, msk, -TWO_PI, tq, ALU.mult, ALU.add)
            ts1(msk, tq, -math.pi, ALU.is_lt)
            stt(tq, msk, TWO_PI, tq, ALU.mult, ALU.add)
            ac(lambda: AC.activation(out=scs, in_=tq, func=AF.Sin))
            tt(Ar, mag, scs[:, 1, :], ALU.mult)
            tt(Ai, mag, scs[:, 0, :], ALU.mult)
            ts1(pm1, Ar, -1.0, ALU.add)
            tt(t1, lr, lr, ALU.mult)
            tt(t2, li, li, ALU.mult)
            tt(den, t1, t2, ALU.add)
            dv(lambda: V.reciprocal(den, den))
            tt(t1, pm1, lr, ALU.mult)
            tt(t2, Ai, li, ALU.mult)
            tt(t1, t1, t2, ALU.add)
            tt(zr, t1, den, ALU.mult)
            tt(t1, Ai, lr, ALU.mult)
            tt(t2, pm1, li, ALU.mult)
            tt(t1, t1, t2, ALU.subtract)
            tt(zi, t1, den, ALU.mult)
            tt(t1, Ar, Ar, ALU.mult)
            tt(t2, Ai, Ai, ALU.mult)
            tt(t1, t1, t2, ALU.add)
            dv(lambda: V.reciprocal(t1, t1))
            tt(Ir, Ar, t1, ALU.mult)
            tt(Ii, Ai, t1, ALU.mult)
            ts1(Ii, Ii, -1.0, ALU.mult)
            dv(lambda: V.tensor_copy(out=Qr[:, 7, :], in_=zr))
            dv(lambda: V.tensor_copy(out=Qi[:, 7, :], in_=zi))
            dv(lambda: V.memset(Pr[:, 7, :], 1.0))
            dv(lambda: V.memset(Pi[:, 7, :], 0.0))
            for k in range(6, -1, -1):
                cmul(Qr[:, k, :], Qi[:, k, :], Qr[:, k + 1, :], Qi[:, k + 1, :], Ar, Ai)
                cmul(Pr[:, k, :], Pi[:, k, :], Pr[:, k + 1, :], Pi[:, k + 1, :], Ir, Ii)
            cmul(a2r, a2i, Ar, Ai, Ar, Ai)
            cmul(a4r, a4i, a2r, a2i, a2r, a2i)
            cmul(self.tabA[:, l, 0, :], self.tabA[:, l, 3, :], a4r, a4i, a4r, a4i)
            dv(lambda l=l: V.tensor_copy(out=self.tabA[:, l, 1, :], in_=self.tabA[:, l, 0, :]))
            ts1(self.tabA[:, l, 2, :], self.tabA[:, l, 3, :], -1.0, ALU.mult)
            with nc.allow_non_contiguous_dma(reason="64B runs"):
                S.dma('sp', Bst[0], W["s5_b_re"][l].rearrange("(sc two) n c -> (two n) sc c", two=2),
                      reads=PRO, writes=PRO)
                S.dma('sp', Bst[1], W["s5_b_im"][l].rearrange("(sc two) n c -> (two n) sc c", two=2),
                      reads=PRO, writes=PRO)
                S.dma('sp', Cn[0], W["s5_c_re"][l].rearrange("(uc g8) c n -> (g8 c) uc n", g8=8),
                      reads=PRO, writes=PRO)
                S.dma('sp', Cn[1], W["s5_c_im"][l].rearrange("(uc g8) c n -> (g8 c) uc n", g8=8),
                      reads=PRO, writes=PRO)
            for k in range(8):
                qr = Qr[:, k, :].unsqueeze(2).broadcast_to([128, 32, 16])
                qi = Qi[:, k, :].unsqueeze(2).broadcast_to([128, 32, 16])
                o1 = Bt[0][:, k]
                o2 = Bt[1][:, k]
                tta = RM_t1[0]
                ttb = RM_t1[1]
                tt(tta, qr, Bst[0], ALU.mult)
                tt(ttb, qi, Bst[1], ALU.mult)
                tt(o1, tta, ttb, ALU.subtract)
                tt(tta, qr, Bst[1], ALU.mult)
                tt(ttb, qi, Bst[0], ALU.mult)
                tt(o2, tta, ttb, ALU.add)
            mI = maskI.unsqueeze(3).broadcast_to([128, 4, 8, 16])
            n_bd = 0
            for uc in range(8):
                for k in range(8):
                    kh, kk = k // 4, k % 4
                    for ri in range(2):
                        b_i = n_bd % 2
                        n_bd += 1
                        bdt = bd[b_i]
                        src = Bt[ri][:, k, uc * 4:(uc + 1) * 4, :].unsqueeze(2).broadcast_to([128, 4, 8, 16])
                        S.op('dve', lambda bdt=bdt, src=src: V.tensor_tensor(out=bdt, in0=src, in1=mI, op=ALU.mult),
                             reads=PRO, writes=[('bd', b_i)])
                        bi = self.bank()
                        pb = self.banks[bi]
                        S.group('pe', [
                            (lambda i=i, pb=pb, bdt=bdt: nc.tensor.transpose(
                                pb[:, i * 128:(i + 1) * 128], bdt[:, i].rearrange("p a b -> p (a b)"),
                                self.ident[:, :])) for i in range(4)],
                            reads=[('bd', b_i), 'ident'], writes=[('bank', bi)])
                        u = ust[(kh, ri)]
                        S.op('act', lambda pb=pb, u=u, kk=kk: AC.copy(
                            out=u[:, :, kk, :], in_=pb[:, 0:512].rearrange("p (i s) -> p i s", s=128)),
                            writes=[('bank', bi), ('ust', kh, ri)])
                for kh in range(2):
                    for ri in range(2):
                        S.dma('sp', self.s5bt[l, uc, kh, ri], ust[(kh, ri)].rearrange("p a b c -> p (a b c)"),
                              reads=[('ust', kh, ri)], writes=['s5scr'])
            m2v = mask2.rearrange("p (i two) -> p i two", two=2).unsqueeze(3).broadcast_to([128, 4, 2, 64])
            n_cd = 0
            for uc in range(8):
                for ri in range(2):
                    c_i = n_cd % 2
                    n_cd += 1
                    cdt = cd[c_i]
                    src = Cn[ri][:, uc, :].unsqueeze(1).unsqueeze(1).broadcast_to([128, 4, 2, 64])
                    S.op('dve', lambda cdt=cdt, src=src: V.tensor_tensor(out=cdt, in0=src, in1=m2v, op=ALU.mult),
                         reads=PRO, writes=[('cd', c_i)])
                    bi = self.bank()
                    pb = self.banks[bi]
                    S.group('pe', [
                        (lambda i=i, pb=pb, cdt=cdt: nc.tensor.transpose(
                            pb[:, i * 128:(i + 1) * 128], cdt[:, i].rearrange("p a b -> p (a b)"),
                            self.ident[:, :])) for i in range(4)],
                        reads=[('cd', c_i), 'ident'], writes=[('bank', bi)])
                    S.op('act', lambda pb=pb, uc=uc, ri=ri: AC.copy(
                        out=CT[ri][:, uc * 4:(uc + 1) * 4, :], in_=pb[:, 0:512].rearrange("p (i s) -> p i s", s=128)),
                        reads=PRO, writes=[('bank', bi)] + PRO)
            for uc in range(8):
                sl4 = slice(uc * 4, (uc + 1) * 4)
                for k in range(8):
                    kh, kk = k // 4, k % 4
                    pr = Pr[:, k, sl4].unsqueeze(2).broadcast_to([128, 4, 128])
                    pi = Pi[:, k, sl4].unsqueeze(2).broadcast_to([128, 4, 128])
                    tt(ct1, pr, CT[0][:, sl4, :], ALU.mult)
                    tt(ct2, pi, CT[1][:, sl4, :], ALU.mult)
                    S.op('dve', lambda kh=kh, kk=kk: V.tensor_tensor(
                        out=ustC[(kh, 0)][:, :, kk, :], in0=ct1, in1=ct2, op=ALU.subtract),
                        reads=PRO, writes=PRO + [('ust', kh, 0)])
                    tt(ct1, pr, CT[1][:, sl4, :], ALU.mult)
                    tt(ct2, pi, CT[0][:, sl4, :], ALU.mult)
                    S.op('dve', lambda kh=kh, kk=kk: V.scalar_tensor_tensor(
                        out=ustC[(kh, 1)][:, :, kk, :], in0=ct1, scalar=-1.0, in1=ct2,
                        op0=ALU.mult, op1=ALU.subtract),
                        reads=PRO, writes=PRO + [('ust', kh, 1)])
                for kh in range(2):
                    for ri in range(2):
                        S.dma('sp', self.s5ct[l, uc, kh, ri], ustC[(kh, ri)].rearrange("p a b c -> p (a b c)"),
                              reads=[('ust', kh, ri)], writes=['s5scr'])
        S.barrier(('pe', 'act', 'dve', 'sp', 'pool'))

    def mixer(self, t, l):
        nc, S = self.nc, self.S
        V, AC = nc.vector, nc.scalar
        W = self.w
        RA, RM = self.RA, self.RM
        RA.reset()
        RM.reset()
        last_tile = (t == self.ntiles - 1)
        self.pre_norm(l, 1)
        xkeys = lambda half: [('xn', c, half) for c in range(KC)]
        xrhs = lambda kc, cs: self.xn[:, kc, cs]
        ymix = RM.alloc([16, TT], BF16)
        m_mark = RM.off
        ubf = RM.alloc([8, TT], BF16)
        ysb = RM.alloc([8, TT], BF16)
        xt = RM.alloc([2, 4, TT], BF16)
        Xse = RM.alloc([2, 4, 32], F32)
        XE = RA.alloc([2, 32, 69], F32)
        sT = [RA.alloc([2, 32], F32) for _ in range(2)]
        sU = [RA.alloc([2, 32], F32) for _ in range(2)]
        sS = [RA.alloc([4, 32], F32) for _ in range(2)]
        A8 = self.tabA[:, l]

        for uc in range(8):
            def ev(half, cs, pb, bi, uc=uc):
                S.op('dve', lambda: V.tensor_copy(out=ubf[:, uc, cs], in_=pb[:, 0:HALF]),
                     writes=[('bank', bi), ('ubf', uc, half)])
            self.proj(W["w_in"][l, :, uc * 128:(uc + 1) * 128], KC, xrhs, xkeys, ev)

        for ri, nm in enumerate(("s5re_in", "s5im_in")):
            si = self.stage_rr % 2
            self.stage_rr += 1
            st = self.sst[si]
            S.dma('sp', st[:, :], W[nm][l, t * TSQ:(t + 1) * TSQ].rearrange("b (sc two) n -> (b sc) (two n)", two=2),
                  writes=[('sst', si)])
            bi = self.bank()
            pb = self.banks[bi]
            S.group('pe', [lambda pb=pb, st=st: nc.tensor.transpose(pb[:, 0:128], st[:, :], self.ident[:, :])],
                    reads=[('sst', si), 'ident'], writes=[('bank', bi)])
            S.op('act', lambda pb=pb, ri=ri: AC.copy(
                out=XE[:, ri, :, 0:4], in_=pb[:, 0:128].rearrange("p (b s) -> p s b", s=32)),
                writes=[('bank', bi), 'XE'])
        S.op('dve', lambda: V.tensor_copy(out=XE[:, :, :, 4], in_=self.s5carry[:, l]),
             reads=['s5carry'], writes=['XE'])

        for uc in range(8):
            sl4 = slice(uc * 4, (uc + 1) * 4)
            un = {(kh, ri): self.load_s5unit(self.s5bt[l, uc, kh, ri]) for kh in range(2) for ri in range(2)}
            uv = ubf[:, uc, :].rearrange("p (c k) -> p c k", k=8)
            for ri in range(2):
                bi = self.bank()
                pb = self.banks[bi]
                fns = []
                for i in range(4):
                    for k in range(8):
                        sl = self.slots[un[(k // 4, ri)]]
                        fns.append(lambda i=i, k=k, sl=sl, pb=pb: nc.tensor.matmul(
                            pb[:, i * NT:(i + 1) * NT], sl[:, i * 4 + k % 4, :], uv[:, :, k],
                            start=(k == 0), stop=(k == 7)))
                S.group('pe', fns, reads=[('slot', un[(kh, ri)]) for kh in range(2)] +
                        [('ubf', uc, 0), ('ubf', uc, 1)], writes=[('bank', bi)])
                pv = pb[:, 0:4 * NT].rearrange("p (i c) -> p i c", c=NT)
                S.op('act', lambda pv=pv, ri=ri, sl4=sl4: AC.copy(out=XE[:, ri, sl4, 5:69], in_=pv[:, :, 4:NT]),
                     writes=[('bank', bi), 'XE'])
                S.op('act', lambda pv=pv, ri=ri, sl4=sl4: AC.copy(
                    out=Xse[:, ri, :, sl4], in_=pv[:, :, 0:4].rearrange("p i b -> p b i")),
                    writes=[('bank', bi), 'Xse'])

        def tt(o, a, b, op, r, w):
            S.op('dve', lambda: V.tensor_tensor(out=o, in0=a, in1=b, op=op), reads=r, writes=w)
        CA = A8[:, 0:2, :]
        for b in range(TSQ):
            ss = sS[b % 2]
            tt(ss[:, 0:2, :], CA, XE[:, :, :, b], ALU.mult, ['XE'], [('sS', b % 2)])
            tt(ss[:, 2, :], A8[:, 2, :], XE[:, 1, :, b], ALU.mult, ['XE'], [('sS', b % 2)])
            tt(ss[:, 3, :], A8[:, 3, :], XE[:, 0, :, b], ALU.mult, ['XE'], [('sS', b % 2)])
            tt(Xse[:, :, b, :], Xse[:, :, b, :], ss[:, 0:2, :], ALU.add, [('sS', b % 2), 'Xse'], ['Xse'])
            tt(Xse[:, :, b, :], Xse[:, :, b, :], ss[:, 2:4, :], ALU.add, [('sS', b % 2), 'Xse'], ['Xse'])
        for idx in range(4, 68):
            ss = sS[idx % 2]
            kx = ('sS', idx % 2)
            tt(ss[:, 0:2, :], CA, XE[:, :, :, idx], ALU.mult, ['XE'], [kx])
            tt(ss[:, 2, :], A8[:, 2, :], XE[:, 1, :, idx], ALU.mult, ['XE'], [kx])
            tt(ss[:, 3, :], A8[:, 3, :], XE[:, 0, :, idx], ALU.mult, ['XE'], [kx])
            tt(XE[:, :, :, idx + 1], XE[:, :, :, idx + 1], ss[:, 0:2, :], ALU.add, [kx, 'XE'], ['XE'])
            tt(XE[:, :, :, idx + 1], XE[:, :, :, idx + 1], ss[:, 2:4, :], ALU.add, [kx, 'XE'], ['XE'])
        S.op('dve', lambda: V.tensor_copy(out=self.s5carry[:, l], in_=XE[:, :, :, 68]),
             reads=['XE'], writes=['s5carry'])
        for ri in range(2):
            bi = self.bank()
            pb = self.banks[bi]
            S.group('pe', [lambda pb=pb, ri=ri: nc.tensor.transpose(
                pb[:, 0:128], Xse[:, ri].rearrange("p b s -> p (b s)"), self.ident[:, :])],
                reads=['Xse', 'ident'], writes=[('bank', bi)])
            si = self.stage_rr % 2
            self.stage_rr += 1
            st = self.sst[si]
            S.op('act', lambda pb=pb, st=st: AC.copy(out=st[:, :], in_=pb[:, 0:128]),
                 writes=[('bank', bi), ('sst', si)])
            ok = self.key('o_s5s')
            S.dma('sp', self.o_s5s[ri][l, t * TSQ:(t + 1) * TSQ].rearrange("b (sc two) n -> (b sc) (two n)", two=2),
                  st[:, :], reads=[('sst', si)], writes=[ok])
            self.out_keys.append(ok)
            if last_tile:
                bi = self.bank()
                pb = self.banks[bi]
                S.group('pe', [lambda pb=pb, ri=ri: nc.tensor.transpose(
                    pb[0:32, 0:128], XE[:, ri, :, 68], self.ident[:, :])],
                    reads=['XE', 'ident'], writes=[('bank', bi)])
                si = self.stage_rr % 2
                self.stage_rr += 1
                st = self.sst[si]
                S.op('act', lambda pb=pb, st=st: AC.copy(out=st[0:32, :], in_=pb[0:32, 0:128]),
                     writes=[('bank', bi), ('sst', si)])
                ok = self.key('o_s5p')
                S.dma('sp', self.o_s5p[ri][l].rearrange("(sc two) n -> sc (two n)", two=2), st[0:32, :],
                      reads=[('sst', si)], writes=[ok])
                self.out_keys.append(ok)

        for uc in range(8):
            un = {(kh, ri): self.load_s5unit(self.s5bt[l, uc, kh, ri]) for kh in range(2) for ri in range(2)}
            for i in range(4):
                sc = uc * 4 + i
                for ri in range(2):
                    for half in range(2):
                        cs = slice(half * HALF, (half + 1) * HALF)
                        bi = self.bank()
                        pb = self.banks[bi]
                        hv = pb[:, 0:HALF].rearrange("p (c k) -> p c k", k=8)
                        uv = ubf[:, uc, cs].rearrange("p (c k) -> p c k", k=8)
                        fns = [(lambda k=k, hv=hv, uv=uv: nc.tensor.matmul(
                            hv[:, :, k], self.slots[un[(k // 4, ri)]][:, i * 4 + k % 4, :], uv[:, :, k],
                            start=True, stop=True)) for k in range(8)]
                        S.group('pe', fns, reads=[('slot', un[(kh, ri)]) for kh in range(2)] + [('ubf', uc, half)],
                                writes=[('bank', bi)])
                        c0 = half * 34
                        pr_ = XE[:, 0, sc, c0:c0 + 34]
                        pi_ = XE[:, 1, sc, c0:c0 + 34]
                        if ri == 0:
                            ops = ((pr_, A8[:, 0, sc:sc + 1]), (pi_, A8[:, 2, sc:sc + 1]))
                        else:
                            ops = ((pi_, A8[:, 0, sc:sc + 1]), (pr_, A8[:, 3, sc:sc + 1]))
                        for (src, scal) in ops:
                            S.op('dve', lambda hv=hv, src=src, scal=scal: V.scalar_tensor_tensor(
                                out=hv[:, :, 0], in0=src, scalar=scal, in1=hv[:, :, 0],
                                op0=ALU.mult, op1=ALU.add), reads=['XE'], writes=[('bank', bi)])
                        S.op('dve', lambda pb=pb, ri=ri, i=i, cs=cs: V.tensor_tensor_scan(
                            out=xt[:, ri, i, cs], data0=self.mask8[:, cs], data1=pb[:, 0:HALF], initial=0.0,
                            op0=ALU.mult, op1=ALU.add), writes=[('bank', bi), ('xt', ri, i, half)])
            uc_ = {(kh, ri): self.load_s5unit(self.s5ct[l, uc, kh, ri]) for kh in range(2) for ri in range(2)}
            for half in range(2):
                cs = slice(half * HALF, (half + 1) * HALF)
                bi = self.bank()
                pb = self.banks[bi]
                hv = pb[:, 0:HALF].rearrange("p (c k) -> p c k", k=8)
                fns = []
                for k in range(8):
                    for i in range(4):
                        for ri in range(2):
                            xv = xt[:, ri, i, cs].rearrange("p (c k) -> p c k", k=8)
                            fns.append(lambda k=k, i=i, ri=ri, xv=xv, hv=hv: nc.tensor.matmul(
                                hv[:, :, k], self.slots[uc_[(k // 4, ri)]][:, i * 4 + k % 4, :], xv[:, :, k],
                                start=(i == 0 and ri == 0), stop=(i == 3 and ri == 1)))
                S.group('pe', fns, reads=[('slot', s_) for s_ in uc_.values()] +
                        [('xt', ri, i, half) for ri in range(2) for i in range(4)], writes=[('bank', bi)])
                t1i, t1 = self.tmp()
                t2i, t2 = self.tmp()
                dcol = self.fvec[:, 0, l * 8 + uc:l * 8 + uc + 1]
                S.op('dve', lambda t1=t1, pb=pb, cs=cs, dcol=dcol, uc=uc: V.scalar_tensor_tensor(
                    out=t1[:], in0=ubf[:, uc, cs], scalar=dcol, in1=pb[:, 0:HALF], op0=ALU.mult, op1=ALU.add),
                    reads=[('ubf', uc, half)], writes=[('bank', bi), ('tmpa', t1i)])
                S.op('dve', lambda t1=t1, t2=t2: V.tensor_tensor(out=t2[:], in0=t1[:], in1=t1[:], op=ALU.mult),
                     reads=[('tmpa', t1i)], writes=[('tmpa', t2i)])
                S.op('dve', lambda t2=t2: V.tensor_scalar(out=t2[:], in0=t2[:], scalar1=0.044715, scalar2=1.0,
                                                        op0=ALU.mult, op1=ALU.add),
                     reads=[('tmpa', t2i)], writes=[('tmpa', t2i)])
                S.op('dve', lambda t1=t1, t2=t2: V.tensor_tensor(out=t2[:], in0=t2[:], in1=t1[:], op=ALU.mult),
                     reads=[('tmpa', t1i), ('tmpa', t2i)], writes=[('tmpa', t2i)])
                S.op('act', lambda t2=t2: AC.activation(out=t2[:], in_=t2[:], func=AF.Sigmoid, scale=1.5957691216),
                     reads=[('tmpa', t2i)], writes=[('tmpa', t2i)])
                S.op('dve', lambda t1=t1, t2=t2, uc=uc, cs=cs: V.tensor_tensor(
                    out=ysb[:, uc, cs], in0=t1[:], in1=t2[:], op=ALU.mult),
                    reads=[('tmpa', t1i), ('tmpa', t2i)], writes=[('ysb', uc, half)])

        for m in range(8):
            def ev(half, cs, pb, bi, m=m):
                ti, tm = self.tmp()
                S.op('act', lambda: AC.activation(out=tm[:], in_=pb[:, 0:HALF], func=AF.Sigmoid,
                                                  bias=self.fvec[:, 1, l * 8 + m:l * 8 + m + 1]),
                     writes=[('bank', bi), ('tmpa', ti)])
                S.op('dve', lambda: V.tensor_tensor(out=ymix[:, m, cs], in0=ysb[:, m, cs], in1=tm[:], op=ALU.mult),
                     reads=[('tmpa', ti), ('ysb', m, half)], writes=[('ymix', m, half)])
            self.proj(W["s5_w_glu"][l, :, m * 128:(m + 1) * 128], 8, lambda kc, cs: ysb[:, kc, cs],
                      lambda half: [('ysb', c, half) for c in range(8)], ev)

        S.barrier()
        RA.reset()
        RM.reset(m_mark)
        self.hgrn(t, l, ymix, xrhs, xkeys)
        S.barrier()
        for m in range(KC):
            def ev(half, cs, pb, bi, m=m):
                S.op('act', lambda: AC.copy(out=self.abuf[:, m, cs], in_=pb[:, 0:HALF]),
                     writes=[('bank', bi), ('abuf', m, half)])
            self.proj(W["w_out"][l, :, m * 128:(m + 1) * 128], KC, lambda kc, cs: ymix[:, kc, cs],
                      lambda half: [('ymix', c, hf) for c in range(KC) for hf in range(2)], ev)
        self.post_norm(l, 1, 1.0)

    def hgrn(self, t, l, ymix, xrhs, xkeys):
        nc, S = self.nc, self.S
        V, AC = nc.vector, nc.scalar
        W = self.w
        RA, RM = self.RA, self.RM
        last_tile = (t == self.ntiles - 1)
        NH = 2
        F = lambda: RA.alloc([TT], F32)
        hb = [dict(sgf=F(), bb=F(), bb2=F(), eb=F(), enb=F(), qs=F(), vf=F(), gs=F()) for _ in range(NH)]
        hg1 = RM.alloc([TT], F32)
        rso = RM.alloc([TT], F32)
        Sall = RM.alloc([8, 128], F32)
        for s_ in range(NH):
            d = hb[s_]
            d['Ss'] = RM.alloc([TSQ, 128], F32)
            d['kt'] = RM.alloc([TT], BF16)
            d['qt'] = RM.alloc([TT], BF16)
            d['sqo'] = RM.alloc([TT], BF16)
            d['Sbf'] = [RM.alloc([128], BF16) for _ in range(3)]
            d['att'] = [RM.alloc([HC], BF16) for _ in range(3)]
            d['qtB'] = RM.alloc([TT], BF16)
            d['kv'] = [RM.alloc([256], BF16) for _ in range(3)]
            d['oA'], d['oAk'] = self.banks[4 + 2 * s_], 4 + 2 * s_
            d['oB'], d['oBk'] = self.banks[5 + 2 * s_], 5 + 2 * s_
        if t == 0:
            S.op('dve', lambda: V.memset(Sall, 0.0), writes=['Sall'])
        else:
            S.dma('sp', Sall.rearrange("p a b -> p (a b)"), self.hgscr[l], reads=[('hgscr', l)], writes=['Sall'])
        self.nbanks_rot = 4
        for hp in range(8 // NH):
            for s_ in range(NH):
                d = hb[s_]
                hd = hp * NH + s_
                col = l * 8 + hd
                base = 1024 + hd * 128
                K_ = lambda nm, *a: (nm, s_) + a
                for (off, func, dst, nm) in ((1024, AF.Sigmoid, d['sgf'], 'sgf'), (0, AF.Silu, d['qs'], 'qs'),
                                              (2048, None, d['vf'], 'vf'), (3072, AF.Silu, d['gs'], 'gs')):
                    def ev(half, cs, pb, bi, func=func, dst=dst, nm=nm, s_=s_):
                        if func is None:
                            S.op('act', lambda: AC.copy(out=dst[:, cs], in_=pb[:, 0:HALF]),
                                 writes=[('bank', bi), (nm, s_, half)])
                        else:
                            S.op('act', lambda: AC.activation(out=dst[:, cs], in_=pb[:, 0:HALF], func=func),
                                 writes=[('bank', bi), (nm, s_, half)])
                    self.proj(W["w_in"][l, :, base + off:base + off + 128], KC, xrhs, xkeys, ev)
                both = lambda nm: [(nm, s_, 0), (nm, s_, 1)]
                sgf, bb, bb2, eb, enb, qs, kt, qt = (d['sgf'], d['bb'], d['bb2'], d['eb'], d['enb'], d['qs'],
                                                     d['kt'], d['qt'])
                S.op('dve', lambda: V.tensor_scalar(
                    out=sgf, in0=sgf, scalar1=self.fvec[:, 4, col:col + 1], scalar2=self.fvec[:, 3, col:col + 1],
                    op0=ALU.mult, op1=ALU.add), reads=both('sgf'), writes=both('sgf'))
                S.op('act', lambda: AC.activation(out=bb, in_=sgf, func=AF.Ln), reads=both('sgf'), writes=[K_('bb')])
                S.op('dve', lambda: V.tensor_tensor_scan(out=bb2, data0=self.mask32[:], data1=bb, initial=0.0,
                                                         op0=ALU.mult, op1=ALU.add), reads=[K_('bb')], writes=[K_('bb2')])
                S.op('act', lambda: AC.activation(out=eb, in_=bb2, func=AF.Exp), reads=[K_('bb2')], writes=[K_('eb')])
                bq = bb2[:, TS:TT].rearrange("p (c k) -> p c k", k=HC)
                S.op('dve', lambda: V.tensor_tensor(
                    out=bb[:, TS:TT].rearrange("p (c k) -> p c k", k=HC), in0=bq,
                    in1=bq[:, :, HC // 2 - 1:HC // 2].broadcast_to([128, NPC, HC]), op=ALU.subtract),
                    reads=[K_('bb2')], writes=[K_('bb')])
                S.op('dve', lambda: V.tensor_copy(out=bb[:, 0:TS], in_=bb2[:, 0:TS]),
                     reads=[K_('bb2')], writes=[K_('bb')])
                S.op('act', lambda: AC.activation(out=enb, in_=bb, func=AF.Exp, scale=-1.0),
                     reads=[K_('bb')], writes=[K_('enb')])
                S.op('act', lambda: AC.activation(out=bb, in_=bb, func=AF.Exp), reads=[K_('bb')], writes=[K_('bb')])
                S.op('dve', lambda: V.tensor_scalar(out=sgf, in0=sgf, scalar1=-1.0, scalar2=1.0,
                                                    op0=ALU.mult, op1=ALU.add), reads=both('sgf'), writes=both('sgf'))
                S.op('dve', lambda: V.tensor_tensor(out=kt, in0=sgf, in1=enb, op=ALU.mult),
                     reads=both('sgf') + [K_('enb')], writes=[K_('kt')])
                S.op('dve', lambda: V.tensor_tensor(out=qt, in0=qs, in1=eb, op=ALU.mult),
                     reads=both('qs') + [K_('eb')], writes=[K_('qt')])
                S.op('dve', lambda: V.tensor_tensor(out=d['qtB'], in0=qs, in1=bb, op=ALU.mult),
                     reads=both('qs') + [K_('bb')], writes=[K_('qtB')])
                b_s = bb2[:, 0:TS].rearrange("p (b k) -> p b k", k=8)
                b_p = bb2[:, TS:TT].rearrange("p (c k) -> p c k", k=HC)
                S.op('dve', lambda: V.tensor_tensor(
                    out=enb[:, 0:TS].rearrange("p (b k) -> p b k", k=8),
                    in0=b_s[:, :, 7:8].broadcast_to([128, TSQ, 8]), in1=b_s, op=ALU.subtract),
                    reads=[K_('bb2')], writes=[K_('enb')])
                S.op('dve', lambda: V.tensor_tensor(
                    out=enb[:, TS:TT].rearrange("p (c k) -> p c k", k=HC),
                    in0=b_p[:, :, HC - 1:HC].broadcast_to([128, NPC, HC]), in1=b_p, op=ALU.subtract),
                    reads=[K_('bb2')], writes=[K_('enb')])
                S.op('act', lambda: AC.activation(out=enb, in_=enb, func=AF.Exp), reads=[K_('enb')], writes=[K_('enb')])
                S.op('dve', lambda: V.tensor_tensor(out=enb, in0=enb, in1=sgf, op=ALU.mult),
                     reads=[K_('enb')] + both('sgf'), writes=[K_('enb')])
                S.dma('sp', d['Ss'], W["hg_in"][l, t * TSQ:(t + 1) * TSQ, hd].rearrange("b k v -> k b v"),
                      writes=[K_('Ss')])
                d['chunks'] = [(b * 8, 8, d['Ss'][:, b, :], K_('Ss')) for b in range(TSQ)] + \
                              [(TS + j * HC, HC, Sall[:, hd, :], ('Sall', hd)) for j in range(NPC)]
            nchunk = TSQ + NPC

            def stage1(s_, idx):
                d = hb[s_]
                c0, C, St, skey = d['chunks'][idx]
                cc = slice(c0, c0 + C)
                ci = idx % 3
                if C == 8 or idx == TSQ:
                    S.op('act', lambda: AC.copy(out=d['Sbf'][ci], in_=St), reads=[skey, 'Sall'],
                         writes=[('Sbf', s_, ci)])
                bi = self.bank()
                pb = self.banks[bi]
                S.group('pe', [
                    lambda: nc.tensor.matmul(pb[0:C, 0:C], d['kt'][:, cc], d['qtB'][:, cc], start=True, stop=True),
                    lambda: nc.tensor.transpose(pb[0:C, 128:256], d['enb'][:, cc], self.ident[:, :]),
                    lambda: nc.tensor.transpose(pb[0:C, 256:384], d['vf'][:, cc], self.ident[:, :])],
                    reads=[('kt', s_), ('qtB', s_), ('enb', s_), ('vf', s_, 0), ('vf', s_, 1), 'ident'],
                    writes=[('bank', bi)])
                S.op('dve', lambda: V.tensor_tensor(
                    out=d['att'][ci][0:C, 0:C], in0=pb[0:C, 0:C], in1=self.maskc[0:C, 0:C], op=ALU.mult),
                    reads=['maskc'], writes=[('bank', bi), ('att', s_, ci)])
                S.op('act', lambda: AC.copy(out=d['kv'][ci][0:C, :], in_=pb[0:C, 128:384]),
                     writes=[('bank', bi), ('kv', s_, ci)])

            def stage2(s_, idx):
                d = hb[s_]
                c0, C, St, skey = d['chunks'][idx]
                cc = slice(c0, c0 + C)
                ci = idx % 3
                if c0 < HG_SPLIT:
                    ob, obk, oc = d['oA'], d['oAk'], slice(c0, c0 + C)
                else:
                    ob, obk, oc = d['oB'], d['oBk'], slice(c0 - HG_SPLIT, c0 - HG_SPLIT + C)
                kv, att = d['kv'][ci], d['att'][ci]
                S.group('pe', [
                    lambda: nc.tensor.matmul(ob[:, oc], kv[0:C, 128:256], att[0:C, 0:C], start=True, stop=False),
                    lambda: nc.tensor.matmul(ob[:, oc], d['Sbf'][ci], d['qt'][:, cc], start=False, stop=True)],
                    reads=[('kv', s_, ci), ('att', s_, ci), ('Sbf', s_, ci), ('qt', s_)], writes=[('bank', obk)])
                bi2 = self.bank()
                pb2 = self.banks[bi2]
                S.group('pe', [lambda: nc.tensor.matmul(
                    pb2[:, 0:128], kv[0:C, 0:128], kv[0:C, 128:256], start=True, stop=True)],
                    reads=[('kv', s_, ci)], writes=[('bank', bi2)])
                S.op('dve', lambda: V.scalar_tensor_tensor(
                    out=St, in0=St, scalar=d['eb'][:, c0 + C - 1:c0 + C], in1=pb2[:, 0:128],
                    op0=ALU.mult, op1=ALU.add), reads=[('eb', s_), skey, 'Sall'], writes=[('bank', bi2), skey])
                if C == HC and idx != nchunk - 1:
                    S.op('act', lambda: AC.copy(out=d['Sbf'][(idx + 1) % 3], in_=St), reads=[skey],
                         writes=[('Sbf', s_, (idx + 1) % 3)])

            for idx in range(nchunk + 1):
                for s_ in range(NH):
                    if idx < nchunk:
                        stage1(s_, idx)
                for s_ in range(NH):
                    if idx >= 1:
                        stage2(s_, idx - 1)

            for s_ in range(NH):
                d = hb[s_]
                hd = hp * NH + s_
                col = l * 8 + hd
                ok = self.key('o_hgs')
                S.dma('sp', self.o_hgs[l, t * TSQ:(t + 1) * TSQ, hd].rearrange("b k v -> k b v"), d['Ss'],
                      reads=[('Ss', s_)], writes=[ok])
                self.out_keys.append(ok)
                gcol = self.fvec[:, 2, col:col + 1]
                sqo, gs = d['sqo'], d['gs']
                for (ob, obk, c0, n) in ((d['oA'], d['oAk'], 0, HG_SPLIT), (d['oB'], d['oBk'], HG_SPLIT, TT - HG_SPLIT)):
                    cc = slice(c0, c0 + n)
                    S.op('act', lambda: AC.activation(out=sqo[:, cc], in_=ob[:, 0:n], func=AF.Square),
                         writes=[('bank', obk), ('sqo', s_)])
                    bi = self.bank()
                    pb = self.banks[bi]
                    S.group('pe', [lambda: nc.tensor.matmul(
                        pb[:, 0:n], self.ones_b[:], sqo[:, cc], start=True, stop=True)],
                        reads=[('sqo', s_), 'ones_b'], writes=[('bank', bi)])
                    S.op('act', lambda: AC.activation(
                        out=hg1[:, cc], in_=pb[:, 0:n], func=AF.Sqrt, bias=self.epsb[:], scale=1.0 / 128),
                        reads=['epsb'], writes=[('bank', bi), 'hg1'])
                    S.op('dve', lambda: V.reciprocal(rso[:, cc], hg1[:, cc]), reads=['hg1'], writes=['rso'])
                    S.op('dve', lambda: V.scalar_tensor_tensor(
                        out=hg1[:, cc], in0=ob[:, 0:n], scalar=gcol, in1=rso[:, cc], op0=ALU.mult, op1=ALU.mult),
                        reads=['rso'], writes=[('bank', obk), 'hg1'])
                    S.op('dve', lambda: V.tensor_tensor(
                        out=ymix[:, 8 + hd, cc], in0=hg1[:, cc], in1=gs[:, cc], op=ALU.mult),
                        reads=['hg1', ('gs', s_, 0), ('gs', s_, 1)],
                        writes=[('ymix', 8 + hd, 0), ('ymix', 8 + hd, 1)])
        self.nbanks_rot = 8
        allS = [('Sall', hd) for hd in range(8)] + ['Sall']
        if last_tile:
            ok = self.key('o_hgp')
            S.dma('sp', self.o_hgp[l].rearrange("h k v -> k h v"), Sall, reads=allS, writes=[ok])
            self.out_keys.append(ok)
        else:
            S.dma('sp', self.hgscr[l], Sall.rearrange("p a b -> p (a b)"), reads=allS, writes=[('hgscr', l)])


def build_program(cfg):
    p = Prog(cfg)
    nc = p.build()
    return p, nc


_WNAMES = ["norm_pre", "norm_post", "ffn1_w_gate", "ffn1_w_up", "ffn1_w_down", "ffn2_w_gate", "ffn2_w_up",
           "ffn2_w_down", "w_in", "w_out", "s5_w_glu", "s5_lam_re", "s5_lam_im", "s5_log_dt", "s5_b_re",
           "s5_b_im", "s5_c_re", "s5_c_im", "s5_d", "s5_b_glu", "hgrn_lb", "hgrn_norm"]


def make_in_maps(inputs, ncores=NCORES):
    f = lambda a: np.ascontiguousarray(np.asarray(a, dtype=np.float32))
    shared = {k: f(inputs[k]) for k in _WNAMES}
    xp, xs = f(inputs["x_prompt"]), f(inputs["x_sample"])
    sre, sim, shg = f(inputs["state_s5_re"]), f(inputs["state_s5_im"]), f(inputs["state_hgrn"])
    maps = []
    for c in range(ncores):
        m = dict(shared)
        m["xp"] = xp[c % 4]
        sl = slice(c * NSAMP, (c + 1) * NSAMP)
        m["xs"] = xs[sl].reshape(NSAMP * DSEQ, D)
        m["s5re_in"] = np.ascontiguousarray(sre[:, sl])
        m["s5im_in"] = np.ascontiguousarray(sim[:, sl])
        m["hg_in"] = np.ascontiguousarray(shg[:, sl])
        maps.append(m)
    return maps


def kernel(**inputs):
    p, nc = build_program({})
    maps = make_in_maps(inputs)
    res = run_bass_kernel_spmd(nc, maps, core_ids=list(range(NCORES)))
    r = res.results
    y_prompt = np.stack([r[c]["yp"] for c in range(4)], 0)
    y_sample = np.concatenate([r[c]["ys"].reshape(NSAMP, DSEQ, D) for c in range(NCORES)], 0)
    re_p = np.stack([r[c]["s5re_p"] for c in range(4)], 1)
    im_p = np.stack([r[c]["s5im_p"] for c in range(4)], 1)
    hg_p = np.stack([r[c]["hg_p"] for c in range(4)], 1)
    re_s = np.concatenate([r[c]["s5re_s"] for c in range(NCORES)], 1)
    im_s = np.concatenate([r[c]["s5im_s"] for c in range(NCORES)], 1)
    hg_s = np.concatenate([r[c]["hg_s"] for c in range(NCORES)], 1)
    return tuple(np.ascontiguousarray(a, dtype=np.float32) for a in
                 (y_prompt, y_sample, re_p, im_p, hg_p, re_s, im_s, hg_s))
```
